# Optimizing a Trainium2 kernel written in Bass

```python
import math
import jax, jax.numpy as jnp
from jax import lax
import numpy as np

D_MODEL = 1024
BATCH = 8
SEQ = 2048
DEPTH = 2
DEC_BATCH = 128
DEC_SEQ = 8
PAST_LEN = 16384
PAGE_SIZE = 128

N_MIXERS = 2
N_SSM_LAYERS = (DEPTH + 1) // 2
N_CONV_LAYERS = DEPTH // 2
GROUP_SIZE = 16
N_GROUPS = D_MODEL // GROUP_SIZE
P_STATE = 64
D_CONV = D_MODEL
CONV_W = 3
D_FF = 4 * D_MODEL
EPS = 1e-6
DT_MIN = 1e-3
DT_MAX = 1e-1

kernel_name = "s5_shortconv_macaron_decode_step"


def _rmsnorm(x, g):
    xf = x.astype(jnp.float32)
    y = xf * lax.rsqrt(jnp.mean(xf * xf, axis=-1, keepdims=True) + EPS) * g.astype(jnp.float32)
    return y.astype(x.dtype)


def _ffn_half(x, g, w_gu, w_down):
    h = _rmsnorm(x, g)
    gate, up = jnp.split(h @ w_gu, 2, axis=-1)
    return (jax.nn.silu(gate) * up) @ w_down


def _ssm_combine(e1, e2):
    a1r, a1i, b1r, b1i = e1
    a2r, a2i, b2r, b2i = e2
    ar = a2r * a1r - a2i * a1i
    ai = a2r * a1i + a2i * a1r
    br = a2r * b1r - a2i * b1i + b2r
    bi = a2r * b1i + a2i * b1r + b2i
    return (ar, ai, br, bi)


def _s5_mixer(h, s0_re, s0_im, lam_re, lam_im, log_dt, b_re, b_im, c_re, c_im, d_skip, w_glu):
    f32 = jnp.float32
    lam_re = lam_re.astype(f32); lam_im = lam_im.astype(f32)
    dt = jnp.exp(log_dt.astype(f32))[:, None]
    mag = jnp.exp(lam_re * dt)
    lb_re = mag * jnp.cos(lam_im * dt)
    lb_im = mag * jnp.sin(lam_im * dt)
    den = lam_re * lam_re + lam_im * lam_im
    nr = lb_re - 1.0
    ni = lb_im
    f_re = (nr * lam_re + ni * lam_im) / den
    f_im = (ni * lam_re - nr * lam_im) / den
    b_re = b_re.astype(f32); b_im = b_im.astype(f32)
    bb_re = f_re[..., None] * b_re - f_im[..., None] * b_im
    bb_im = f_re[..., None] * b_im + f_im[..., None] * b_re
    c_re = c_re.astype(f32); c_im = c_im.astype(f32)
    d_skip = d_skip.astype(f32)

    def one_seq(args):
        u, sr, si = args
        L = u.shape[0]
        u = u.astype(f32)
        ug = u.reshape(L, N_GROUPS, GROUP_SIZE)
        bu_re = jnp.einsum('gpc,lgc->lgp', bb_re, ug)
        bu_im = jnp.einsum('gpc,lgc->lgp', bb_im, ug)
        sr = sr.astype(f32); si = si.astype(f32)
        bu_re = bu_re.at[0].add(lb_re * sr - lb_im * si)
        bu_im = bu_im.at[0].add(lb_re * si + lb_im * sr)
        a_re = jnp.broadcast_to(lb_re, bu_re.shape)
        a_im = jnp.broadcast_to(lb_im, bu_im.shape)
        _, _, hs_re, hs_im = lax.associative_scan(_ssm_combine, (a_re, a_im, bu_re, bu_im), axis=0)
        y = (jnp.einsum('gcp,lgp->lgc', c_re, hs_re) - jnp.einsum('gcp,lgp->lgc', c_im, hs_im))
        y = y.reshape(L, D_MODEL) + d_skip * u
        return y, hs_re[-1], hs_im[-1]

    y, new_re, new_im = lax.map(one_seq, (h, s0_re, s0_im))
    g = jax.nn.gelu(y)
    ga, gb = jnp.split(g @ w_glu.astype(f32), 2, axis=-1)
    return ga * jax.nn.sigmoid(gb), new_re, new_im


def _conv_mixer(h, buf, w_in, conv_w, w_out):
    gb, gc, v = jnp.split(h @ w_in, 3, axis=-1)
    z = gc * v
    zp = jnp.concatenate([buf.astype(z.dtype), z], axis=1)
    L = h.shape[1]
    conv = conv_w[0] * zp[:, 0:L]
    for k in range(1, CONV_W):
        conv = conv + conv_w[k] * zp[:, k:k + L]
    return (gb * conv) @ w_out, zp[:, zp.shape[1] - (CONV_W - 1):]


def setup_inputs(seed: int = 0) -> dict:
    key = jax.random.key(seed)
    ks = jax.random.split(key, 24)
    f32 = jnp.float32
    x_prompt = jax.random.normal(ks[0], (BATCH, SEQ, D_MODEL), f32)
    x_sample = jax.random.normal(ks[1], (DEC_BATCH, DEC_SEQ, D_MODEL), f32)
    state_ssm_re = 0.5 * jax.random.normal(ks[2], (N_SSM_LAYERS, DEC_BATCH, N_GROUPS, P_STATE), f32)
    state_ssm_im = 0.5 * jax.random.normal(ks[3], (N_SSM_LAYERS, DEC_BATCH, N_GROUPS, P_STATE), f32)
    cache_conv = 0.5 * jax.random.normal(ks[4], (N_CONV_LAYERS, DEC_BATCH, CONV_W - 1, D_CONV), f32)

    norm_g = 1.0 + 0.05 * jax.random.normal(ks[5], (DEPTH, 3, D_MODEL), f32)
    final_norm_g = 1.0 + 0.05 * jax.random.normal(ks[6], (D_MODEL,), f32)
    ffn_w_gate_up = jax.random.normal(ks[7], (DEPTH, 2, D_MODEL, 2 * D_FF), f32) * D_MODEL ** -0.5
    ffn_w_down = jax.random.normal(ks[8], (DEPTH, 2, D_FF, D_MODEL), f32) * D_FF ** -0.5

    n = jnp.arange(P_STATE, dtype=f32)
    ssm_lam_re = -0.5 + 0.01 * jax.random.normal(ks[9], (N_SSM_LAYERS, N_GROUPS, P_STATE), f32)
    ssm_lam_im = math.pi * n + 0.01 * jax.random.normal(ks[10], (N_SSM_LAYERS, N_GROUPS, P_STATE), f32)
    ssm_log_dt = jax.random.uniform(ks[11], (N_SSM_LAYERS, N_GROUPS), f32,
                                    minval=math.log(DT_MIN), maxval=math.log(DT_MAX))
    bscale = (2.0 * GROUP_SIZE) ** -0.5
    ssm_b_re = jax.random.normal(ks[12], (N_SSM_LAYERS, N_GROUPS, P_STATE, GROUP_SIZE), f32) * bscale
    ssm_b_im = jax.random.normal(ks[13], (N_SSM_LAYERS, N_GROUPS, P_STATE, GROUP_SIZE), f32) * bscale
    cscale = (2.0 * P_STATE) ** -0.5
    ssm_c_re = jax.random.normal(ks[14], (N_SSM_LAYERS, N_GROUPS, GROUP_SIZE, P_STATE), f32) * cscale
    ssm_c_im = jax.random.normal(ks[15], (N_SSM_LAYERS, N_GROUPS, GROUP_SIZE, P_STATE), f32) * cscale
    ssm_d = jax.random.normal(ks[16], (N_SSM_LAYERS, D_MODEL), f32)
    ssm_w_glu = jax.random.normal(ks[17], (N_SSM_LAYERS, D_MODEL, 2 * D_MODEL), f32) * D_MODEL ** -0.5

    conv_w_in = jax.random.normal(ks[18], (N_CONV_LAYERS, D_MODEL, 3 * D_CONV), f32) * D_MODEL ** -0.5
    conv_w = jax.random.normal(ks[19], (N_CONV_LAYERS, CONV_W, D_CONV), f32) * CONV_W ** -0.5
    conv_w_out = jax.random.normal(ks[20], (N_CONV_LAYERS, D_CONV, D_MODEL), f32) * D_CONV ** -0.5
    return {"x_prompt": x_prompt, "x_sample": x_sample,
            "state_ssm_re": state_ssm_re, "state_ssm_im": state_ssm_im, "cache_conv": cache_conv,
            "norm_g": norm_g, "final_norm_g": final_norm_g,
            "ffn_w_gate_up": ffn_w_gate_up, "ffn_w_down": ffn_w_down,
            "ssm_lam_re": ssm_lam_re, "ssm_lam_im": ssm_lam_im, "ssm_log_dt": ssm_log_dt,
            "ssm_b_re": ssm_b_re, "ssm_b_im": ssm_b_im, "ssm_c_re": ssm_c_re, "ssm_c_im": ssm_c_im,
            "ssm_d": ssm_d, "ssm_w_glu": ssm_w_glu,
            "conv_w_in": conv_w_in, "conv_w": conv_w, "conv_w_out": conv_w_out}


def reference(x_prompt, x_sample, state_ssm_re, state_ssm_im, cache_conv,
              norm_g, final_norm_g, ffn_w_gate_up, ffn_w_down,
              ssm_lam_re, ssm_lam_im, ssm_log_dt, ssm_b_re, ssm_b_im, ssm_c_re, ssm_c_im,
              ssm_d, ssm_w_glu, conv_w_in, conv_w, conv_w_out):
    nb = x_prompt.shape[0]
    zero_re = jnp.zeros((N_SSM_LAYERS, nb, N_GROUPS, P_STATE), jnp.float32)
    zero_im = jnp.zeros((N_SSM_LAYERS, nb, N_GROUPS, P_STATE), jnp.float32)
    zero_buf = jnp.zeros((N_CONV_LAYERS, nb, CONV_W - 1, D_CONV), x_prompt.dtype)
    groups = ((x_prompt, zero_re, zero_im, zero_buf),
              (x_sample, state_ssm_re, state_ssm_im, cache_conv))
    outs = []
    for x, s_re, s_im, buf in groups:
        new_re, new_im, new_buf = [], [], []
        for i in range(DEPTH):
            x = x + 0.5 * _ffn_half(x, norm_g[i, 0], ffn_w_gate_up[i, 0], ffn_w_down[i, 0])
            h = _rmsnorm(x, norm_g[i, 1])
            j = i // N_MIXERS
            if i % N_MIXERS == 0:
                m, r, im = _s5_mixer(h, s_re[j], s_im[j], ssm_lam_re[j], ssm_lam_im[j], ssm_log_dt[j],
                                     ssm_b_re[j], ssm_b_im[j], ssm_c_re[j], ssm_c_im[j],
                                     ssm_d[j], ssm_w_glu[j])
                new_re.append(r)
                new_im.append(im)
            else:
                m, b = _conv_mixer(h, buf[j], conv_w_in[j], conv_w[j], conv_w_out[j])
                new_buf.append(b)
            x = x + m.astype(x.dtype)
            x = x + 0.5 * _ffn_half(x, norm_g[i, 2], ffn_w_gate_up[i, 1], ffn_w_down[i, 1])
        y = _rmsnorm(x, final_norm_g)
        outs.append((y, jnp.stack(new_re), jnp.stack(new_im), jnp.stack(new_buf)))
    (y_prompt, sre_p, sim_p, conv_p), (y_sample, sre_s, sim_s, conv_s) = outs
    return (y_prompt, y_sample, sre_p, sim_p, conv_p, sre_s, sim_s, conv_s)
```

```python
import math
import numpy as np
import concourse.bass as bass
import concourse.mybir as mybir
from concourse.bass_utils import run_bass_kernel_spmd

F32 = mybir.dt.float32
F32R = mybir.dt.float32r
BF16 = mybir.dt.bfloat16
AF = mybir.ActivationFunctionType
ALU = mybir.AluOpType

D = 1024
KT = 8
DFF = 4096
NP = 2048
NSQ = 16
LS = 8
NS = NSQ * LS
NT = NP + NS
NPAIR = 32
EPS = 1e-6
NCORES = 8
TWO_PI = 2.0 * math.pi
MAGIC = 12582912.0

NPIECES = [(0, 512), (512, 512), (1024, 512), (1536, 512), (2048, 128)]
NTZ = 2 + NP + NSQ * (2 + LS)
SPIECES = [(0, 1024), (1024, 1024), (2048, 128)]


class _Node:
    __slots__ = ("ch", "w", "r")

    def __init__(self):
        self.ch = {}
        self.w = None
        self.r = []


def _collect(node, ws, rs):
    if node.w is not None:
        ws.append(node.w)
    if node.r:
        rs.extend(node.r)
    for c in node.ch.values():
        _collect(c, ws, rs)


class Prog:
    ENGS = ("pe", "act", "dve", "pool", "sp")

    def __init__(self, nc, nds=16):
        self.nc = nc
        self.eng = {"pe": nc.tensor, "act": nc.scalar, "dve": nc.vector, "pool": nc.gpsimd, "sp": nc.sync}
        self.ops = []
        self.nds = nds

    def add(self, eng, fn, r=(), w=(), dma=False):
        self.ops.append((eng, fn, tuple(r), tuple(w), dma))

    def _touch(self, root, key, ws, rs):
        node = root
        for part in key:
            if node.w is not None:
                ws.append(node.w)
            if node.r:
                rs.extend(node.r)
            nxt = node.ch.get(part)
            if nxt is None:
                nxt = _Node()
                node.ch[part] = nxt
            node = nxt
        _collect(node, ws, rs)
        return node

    def finalize(self):
        import os
        mx = os.environ.get("K_MAXOPS")
        if mx:
            self.ops = self.ops[:int(mx)]
        nc = self.nc
        nds = self.nds
        sems = {e: nc.alloc_semaphore("sem_" + e) for e in self.ENGS}
        dsems = [nc.alloc_semaphore("dsem%d" % i) for i in range(nds)]
        seqc = {e: 0 for e in self.ENGS}
        known_e = {e: {e2: -1 for e2 in self.ENGS} for e in self.ENGS}
        known_d = {e: [0] * nds for e in self.ENGS}
        signaled = {e: set() for e in self.ENGS}
        root = _Node()
        plan = []
        ndma = 0
        for (eng, fn, r, w, dma) in self.ops:
            if dma:
                tok = ("d", ndma)
                ndma += 1
            else:
                tok = ("e", eng, seqc[eng])
                seqc[eng] += 1
            ws, rs = [], []
            rnodes = [self._touch(root, k, ws, rs) for k in r]
            deps = set(ws)
            ws2, rs2 = [], []
            wnodes = [self._touch(root, k, ws2, rs2) for k in w]
            deps.update(ws2)
            deps.update(rs2)
            if dma and tok[1] >= nds:
                deps.add(("d", tok[1] - nds))
            need_e, need_d = {}, {}
            for d in deps:
                if d[0] == "e":
                    if d[1] == eng and eng == "pe":
                        continue
                    if need_e.get(d[1], -1) < d[2]:
                        need_e[d[1]] = d[2]
                else:
                    s = d[1] % nds
                    val = 16 * (d[1] // nds + 1)
                    if need_d.get(s, 0) < val:
                        need_d[s] = val
            waits = []
            for e2, s2 in need_e.items():
                if known_e[eng][e2] < s2:
                    known_e[eng][e2] = s2
                    waits.append(("e", e2, s2))
                    signaled[e2].add(s2)
            for s, val in need_d.items():
                if known_d[eng][s] < val:
                    known_d[eng][s] = val
                    waits.append(("d", s, val))
            plan.append((waits, tok))
            for node in rnodes:
                if tok[0] == "e":
                    node.r = [t for t in node.r if not (t[0] == "e" and t[1] == tok[1])]
                node.r.append(tok)
            for node in wnodes:
                node.w = tok
                node.r = []
                node.ch = {}
        cnt = {}
        for e in self.ENGS:
            cnt[e] = {s: i + 1 for i, s in enumerate(sorted(signaled[e]))}
        self.n_sig = {e: len(cnt[e]) for e in self.ENGS}
        dcount = [0] * nds
        for (eng, fn, r, w, dma), (waits, tok) in zip(self.ops, plan):
            E = self.eng[eng]
            for wt in waits:
                if wt[0] == "e":
                    E.wait_ge(sems[wt[1]], cnt[wt[1]][wt[2]])
                else:
                    E.wait_ge(dsems[wt[1]], wt[2])
            ins = fn()
            if dma:
                s = tok[1] % nds
                ins.then_inc(dsems[s], 16)
                dcount[s] += 1
            elif tok[2] in cnt[eng]:
                ins.then_inc(sems[eng], 1)
        sp = self.eng["sp"]
        for s in range(nds):
            if dcount[s]:
                sp.wait_ge(dsems[s], 16 * dcount[s])


class Builder:
    def __init__(self, stages=("all",)):
        self.stages = stages
        nc = bass.Bass("TRN2", target_bir_lowering=False)
        self.nc = nc
        self.pg = Prog(nc)
        self._bank = 0
        self._wslot = 0
        self._sslot = 0
        self._tslot = 0
        self._declare_dram()
        self._alloc_sbuf()

    def _declare_dram(self):
        nc = self.nc

        def inp(name, shape):
            return nc.dram_tensor(name, list(shape), F32, kind="ExternalInput").ap()

        def outp(name, shape):
            return nc.dram_tensor(name, list(shape), F32, kind="ExternalOutput").ap()

        self.d_xT = inp("xT", (D, NT))
        self.d_wgu = inp("w_gu", (4, D, 2 * DFF))
        self.d_wdn = inp("w_dn", (4, DFF, D))
        self.d_wglu = inp("w_glu", (D, 2 * D))
        self.d_win = inp("w_in", (D, 3 * D))
        self.d_wout = inp("w_out", (D, D))
        self.d_gvec = inp("gvec", (128, 7 * KT))
        self.d_dvec = inp("dvec", (128, KT))
        self.d_convw = inp("convw", (128, 3 * KT))
        self.d_cache = inp("cacheT", (128, KT * NSQ * 2))
        self.d_lam_re = inp("lamT_re", (128, NPAIR))
        self.d_lam_im = inp("lamT_im", (128, NPAIR))
        self.d_ldt = inp("ldtT", (128, NPAIR))
        self.d_bt_re = inp("btc_re", (128, KT * 128))
        self.d_bt_im = inp("btc_im", (128, KT * 128))
        self.d_ct_re = inp("ctc_re", (128, NPAIR * 32))
        self.d_ct_im = inp("ctc_im", (128, NPAIR * 32))
        self.d_s0_re = inp("s0_re", (128, NPAIR * NSQ))
        self.d_s0_im = inp("s0_im", (128, NPAIR * NSQ))
        self.d_consts = inp("consts", (128, 512 + 128 + 4 + 128))
        self.o_yT = outp("yT", (D, NT))
        self.o_sre_p = outp("sre_p", (128, NPAIR))
        self.o_sim_p = outp("sim_p", (128, NPAIR))
        self.o_sre_s = outp("sre_s", (128, NPAIR * NSQ))
        self.o_sim_s = outp("sim_s", (128, NPAIR * NSQ))
        self.o_conv_p = outp("conv_p", (128, KT * 2))
        self.o_conv_s = outp("conv_s", (128, KT * NSQ * 2))

    def _alloc_sbuf(self):
        nc = self.nc
        self.x = nc.alloc_sbuf_tensor("sb_x", [128, KT, NT], F32)
        self.xn = nc.alloc_sbuf_tensor("sb_xn", [128, KT, NT], BF16)
        self.rstd = nc.alloc_sbuf_tensor("sb_rstd", [128, NT], F32)
        self.NSTG = 2
        self.NWBF = 4
        self.stg = [nc.alloc_sbuf_tensor("sb_stg%d" % i, [128, 2048], F32) for i in range(self.NSTG)]
        self.wbf = [nc.alloc_sbuf_tensor("sb_wbf%d" % i, [128, 2048], BF16) for i in range(self.NWBF)]
        self.tmp = [nc.alloc_sbuf_tensor("sb_tmp%d" % i, [128, 512], F32) for i in range(2)]
        self.tmpb = [nc.alloc_sbuf_tensor_at("sb_tmpb%d" % i, [128, 1024], BF16, offset=self._sb_offset(self.tmp[i]))
                     for i in range(2)]
        ARENA = KT * NTZ * 2
        base = nc.alloc_sbuf_tensor("sb_arena1", [128, ARENA // 4], F32)
        off0 = self._sb_offset(base)
        self.hid = nc.alloc_sbuf_tensor_at("sb_hid", [128, 4, NT], BF16, offset=off0)
        self.sq = nc.alloc_sbuf_tensor_at("sb_sq", [128, 2, NT], F32R, offset=off0 + 4 * NT * 2)
        self.z = nc.alloc_sbuf_tensor_at("sb_z", [128, KT, NTZ], BF16, offset=off0)
        self.s5p = []
        for st in range(2):
            o = off0 + st * 16384
            d = {}
            for i, n in enumerate(("A2", "A4")):
                d[n] = nc.alloc_sbuf_tensor_at("sb_s5_%s_%d" % (n, st), [128, 512], F32, offset=o + i * 2048)
            for i, n in enumerate(("pr", "pi", "q1", "q2", "q3", "q4", "g1", "g2", "m1", "m2", "m3", "m4")):
                d[n] = nc.alloc_sbuf_tensor_at("sb_s5_%s_%d" % (n, st), [128, 512], BF16, offset=o + 4096 + i * 1024)
            d["f1"] = nc.alloc_sbuf_tensor_at("sb_s5_f1_%d" % st, [128, NS], F32, offset=o + 1024)
            d["f3"] = nc.alloc_sbuf_tensor_at("sb_s5_f3_%d" % st, [128, NS], F32, offset=o + 2048 + 1024)
            self.s5p.append(d)
        self.s5tab = []
        for st in range(2):
            o = self._sb_offset(self.stg[st])
            d = {"cos32": nc.alloc_sbuf_tensor_at("sb_s5_cos32_%d" % st, [128, 512], F32, offset=o),
                 "sin32": nc.alloc_sbuf_tensor_at("sb_s5_sin32_%d" % st, [128, 512], F32, offset=o + 2048),
                 "cos16": nc.alloc_sbuf_tensor_at("sb_s5_cos16_%d" % st, [128, 512], BF16, offset=o + 4096),
                 "sin16": nc.alloc_sbuf_tensor_at("sb_s5_sin16_%d" % st, [128, 512], BF16, offset=o + 5120),
                 "ang": nc.alloc_sbuf_tensor_at("sb_s5_ang_%d" % st, [128, 512], F32, offset=o + 6144)}
            d["nsin16"] = nc.alloc_sbuf_tensor_at("sb_s5_nsin16_%d" % st, [128, 512], BF16,
                                                  offset=self._sb_offset(self.wbf[st]))
            d["nsl"] = nc.alloc_sbuf_tensor("sb_s5_nsl_%d" % st, [128, 2], F32)
            self.s5tab.append(d)
        self.gvec = nc.alloc_sbuf_tensor("sb_gvec", [128, 7, KT], F32)
        self.dvec = nc.alloc_sbuf_tensor("sb_dvec", [128, KT], F32)
        self.convw = nc.alloc_sbuf_tensor("sb_convw", [128, 3, KT], F32)
        self.cst = nc.alloc_sbuf_tensor("sb_cst", [128, 512 + 128 + 4 + 128], F32)
        self.onesr = nc.alloc_sbuf_tensor("sb_onesr", [128, 128], F32R)
        self.halfpi = nc.alloc_sbuf_tensor("sb_halfpi", [128, 1], F32)
        self.cachef = nc.alloc_sbuf_tensor("sb_cachef", [128, KT, NSQ, 2], F32)
        self.zl_p = nc.alloc_sbuf_tensor("sb_zl_p", [128, KT, 2], F32)
        self.zl_s = nc.alloc_sbuf_tensor("sb_zl_s", [128, KT, NSQ, 2], F32)
        self.s5s = {}
        for n in ["lre", "lim", "dt", "lr", "mag", "th", "thn", "k", "u", "au", "sn", "cs", "lbre", "lbim",
                  "nr", "den", "t1", "t2", "fre", "fim", "rden", "fire", "fiim", "cre", "cim", "ore", "oim"]:
            self.s5s[n] = nc.alloc_sbuf_tensor("sb_s5s_" + n, [128, NPAIR], F32)
        self.si = {n: nc.alloc_sbuf_tensor("sb_si_" + n, [128, NPAIR, NSQ], F32) for n in ("re", "im")}
        self.btc = {n: nc.alloc_sbuf_tensor("sb_btc_" + n, [128, KT, 128], BF16) for n in ("re", "im")}
        self.bpad = {n: [nc.alloc_sbuf_tensor("sb_bpad_%s%d" % (n, i), [128, 128], BF16) for i in range(2)]
                     for n in ("re", "im")}
        self.cft = {n: nc.alloc_sbuf_tensor("sb_cft_" + n, [128, NPAIR, 32], BF16) for n in ("re", "im")}
        self.magmask = [nc.alloc_sbuf_tensor("sb_magmask%d" % i, [128, 128], F32) for i in range(2)]
        self.dm = nc.alloc_sbuf_tensor("sb_dm", [128, KT, 128], BF16)
        self.identb = nc.alloc_sbuf_tensor("sb_identb", [128, 128], BF16)
        odm = self._sb_offset(self.dm)
        self.cdiag = [nc.alloc_sbuf_tensor_at("sb_cdiag%d" % i, [128, 3, 128], BF16, offset=odm + i * 768) for i in range(2)]
        self.cty = [nc.alloc_sbuf_tensor("sb_cty%d" % i, [128, 4], F32) for i in range(2)]
        self.scr = [nc.alloc_sbuf_tensor("sb_scr%d" % i, [128, 2 * NSQ], F32) for i in range(2)]
        self.fence_t = nc.alloc_sbuf_tensor("sb_fence", [128, 2], F32)
        self.ps = nc.alloc_psum_tensor("ps", [128, 8, 512], F32)

    def _sb_offset(self, handle):
        loc = self.nc.lookup_mloc(handle)
        for attr in ("offset", "addr", "address", "start", "byte_offset"):
            if hasattr(loc, attr):
                v = getattr(loc, attr)
                if isinstance(v, int):
                    return v
        raise RuntimeError("cannot find sbuf offset: %r %s" % (loc, dir(loc)))

    def arena_fence(self, extra=()):
        nc = self.nc
        ft = self.fence_t
        self.pg.add("dve", lambda: nc.vector.memset(ft[:], 0.0), w=[("ar",)] + list(extra))

    def bank(self, n=1):
        if n == 2 and self._bank % 2:
            self._bank += 1
        b = self._bank % 8
        self._bank += n
        return b

    def psk(self, b, n=1):
        return [("ps", b + i) for i in range(n)]

    def tslot(self):
        s = self._tslot % len(self.tmp)
        self._tslot += 1
        return s

    def dma(self, out, in_, r=(), w=()):
        nc = self.nc
        self.pg.add("sp", lambda: nc.sync.dma_start(out=out, in_=in_), r=r, w=w, dma=True)

    def load_raw(self, src_ap, view, nelem):
        ss = self._sslot % self.NSTG
        self._sslot += 1
        dst = view(self.stg[ss][:, 0:nelem])
        if isinstance(src_ap, (list, tuple)):
            for i, sa in enumerate(src_ap):
                self.dma(dst[:, i], sa, w=[("stg", ss, i)])
        else:
            self.dma(dst, src_ap, w=[("stg", ss, 0)])
        return ss

    def load_piece(self, src_ap, view, nelem=2048):
        nc = self.nc
        ss = self.load_raw(src_ap, view, nelem)
        s = self._wslot % self.NWBF
        self._wslot += 1
        stg = self.stg[ss]
        wbf = self.wbf[s]
        self.pg.add("pool", lambda: nc.gpsimd.tensor_copy(wbf[:, 0:nelem], stg[:, 0:nelem]),
                    r=[("stg", ss)], w=[("wbf", s)])
        return s

    def stage_load(self):
        nc = self.nc
        xT = self.d_xT.rearrange("(k p) n -> p k n", p=128)
        for n, (n0, nsz) in enumerate(NPIECES):
            self.dma(self.x[:, :, n0:n0 + nsz], xT[:, :, n0:n0 + nsz], w=[("x", ct, n) for ct in range(KT)])
        self.dma(self.gvec[:].rearrange("p a b -> p (a b)"), self.d_gvec, w=[("gvec",)])
        self.dma(self.dvec[:], self.d_dvec, w=[("dvec",)])
        self.dma(self.convw[:].rearrange("p a b -> p (a b)"), self.d_convw, w=[("convw",)])
        self.dma(self.cst[:], self.d_consts, w=[("cst",)])
        ones = self.onesr
        cst = self.cst
        t = self.tslot()
        tm = self.tmp[t]
        self.pg.add("dve", lambda: nc.vector.memset(tm[:, 0:128], 1.0), w=[("tmp", t)])
        self.pg.add("dve", lambda: nc.vector.tensor_copy(ones[:], tm[:, 0:128]), r=[("tmp", t)], w=[("ones",)])
        hp = self.halfpi
        self.pg.add("dve", lambda: nc.vector.memset(hp[:], math.pi / 2.0), w=[("halfpi",)])
        self.pg.add("pool", (lambda: nc.gpsimd.tensor_copy(self.identb[:], self.ident())), r=[("cst",)], w=[("identb",)])

    def iota1(self):
        return self.cst[:, 0:512]

    def mask8(self):
        return self.cst[:, 512:640]

    def rowmask(self, i):
        return self.cst[:, 640 + i:641 + i]

    def ident(self):
        return self.cst[:, 644:772]

    def norm_piece(self, gi, n, final=False):
        nc = self.nc
        pg = self.pg
        x, xn, sq, ones, rstd, gvec, ps = self.x, self.xn, self.sq, self.onesr, self.rstd, self.gvec, self.ps
        n0, nsz = NPIECES[n]
        b = self.bank()
        for ct in range(KT):
            sl = ct % 2
            pg.add("act", (lambda ct=ct, sl=sl: nc.scalar.activation(sq[:, sl, n0:n0 + nsz], x[:, ct, n0:n0 + nsz], AF.Square)),
                   r=[("x", ct, n)], w=[("ar", "sq", sl, n)])
            pg.add("pe", (lambda ct=ct, sl=sl: nc.tensor.matmul(
                ps[:, b, 0:nsz], ones[:], sq[:, sl, n0:n0 + nsz], start=(ct == 0), stop=(ct == KT - 1))),
                r=[("ar", "sq", sl, n), ("ones",)], w=[("ps", b)])
        pg.add("act", (lambda: nc.scalar.activation(
            rstd[:, n0:n0 + nsz], ps[:, b, 0:nsz], AF.Sqrt, bias=EPS, scale=1.0 / D)),
            r=[("ps", b)], w=[("rstd", n)])
        pg.add("dve", (lambda: nc.vector.reciprocal(rstd[:, n0:n0 + nsz], rstd[:, n0:n0 + nsz])),
               r=[("rstd", n)], w=[("rstd", n)])
        for ct in range(KT):
            dst = x if final else xn
            wk = ("x", ct, n) if final else ("xn", ct, n)
            pg.add("dve", (lambda ct=ct, dst=dst: nc.vector.scalar_tensor_tensor(
                dst[:, ct, n0:n0 + nsz], x[:, ct, n0:n0 + nsz], gvec[:, gi, ct:ct + 1],
                rstd[:, n0:n0 + nsz], ALU.mult, ALU.mult)),
                r=[("x", ct, n), ("rstd", n), ("gvec",)], w=[wk])

    def stage_norm(self, gi, final=False):
        for n in range(len(NPIECES)):
            self.norm_piece(gi, n, final)

    def pair_piece_ap(self, w2d, colA, colB):
        v = w2d.rearrange("(k p) n -> p k n", p=128)
        return [v[:, :, colA:colA + 128], v[:, :, colB:colB + 128]]

    @staticmethod
    def pair_view(stg_ap):
        return stg_ap.rearrange("p (h k c) -> p h k c", h=2, k=KT)

    def pair_matmuls(self, s, rhs_fn, rkeys_fn, n, n0, nsz):
        nc = self.nc
        ps = self.ps
        wv = self.wbf[s][:, :].rearrange("p (h k c) -> p h k c", h=2, k=KT)
        outb = []
        for h in range(2):
            b = self.bank()
            outb.append(b)

            def emit(h=h, b=b):
                ins = None
                for k in range(KT):
                    ins = nc.tensor.matmul(ps[:, b, 0:nsz], wv[:, h, k, :], rhs_fn(k, n0, nsz),
                                           start=(k == 0), stop=(k == KT - 1))
                return ins
            self.pg.add("pe", emit, r=[("wbf", s)] + rkeys_fn(n), w=[("ps", b)])
        return outb

    def stage_ffn(self, f, tail=None):
        nc = self.nc
        pg = self.pg
        x, xn, hid, ps = self.x, self.xn, self.hid, self.ps
        wgu = self.d_wgu[f]
        wdn = self.d_wdn[f].rearrange("(k p) n -> p k n", p=128)
        for c in range(DFF // 512):
            for j4 in range(4):
                j = c * 4 + j4
                s = self.load_piece(self.pair_piece_ap(wgu, j * 128, DFF + j * 128), self.pair_view)
                for n, (n0, nsz) in enumerate(NPIECES):
                    bg, bu = self.pair_matmuls(
                        s, lambda k, n0, nsz: xn[:, k, n0:n0 + nsz],
                        lambda n: [("xn", k, n) for k in range(KT)], n, n0, nsz)
                    t = self.tslot()
                    tm = self.tmp[t]
                    pg.add("act", (lambda bg=bg, tm=tm, nsz=nsz: nc.scalar.activation(
                        tm[:, 0:nsz], ps[:, bg, 0:nsz], AF.Silu)), r=[("ps", bg)], w=[("tmp", t)])
                    pg.add("dve", (lambda bu=bu, tm=tm, j4=j4, n0=n0, nsz=nsz: nc.vector.tensor_tensor(
                        hid[:, j4, n0:n0 + nsz], ps[:, bu, 0:nsz], tm[:, 0:nsz], ALU.mult)),
                        r=[("ps", bu), ("tmp", t)], w=[("ar", "hid", j4, n)])
            sd = []
            for i2 in range(2):
                kk0 = c * 4 + i2 * 2
                sd.append(self.load_piece(wdn[:, kk0:kk0 + 2, :],
                                          lambda a: a.rearrange("p (k c) -> p k c", k=2)))
            last = (c == DFF // 512 - 1) and tail is not None
            order = [(m, n) for n in range(len(NPIECES)) for m in range(KT)] if last else \
                    [(m, n) for m in range(KT) for n in range(len(NPIECES))]
            for (m, n) in order:
                n0, nsz = NPIECES[n]
                b = self.bank()

                def emit(m=m, b=b, n0=n0, nsz=nsz, sd=tuple(sd)):
                    ins = None
                    for kk in range(4):
                        wv = self.wbf[sd[kk // 2]][:, :].rearrange("p (k c) -> p k c", k=2)
                        ins = nc.tensor.matmul(ps[:, b, 0:nsz], wv[:, kk % 2, m * 128:(m + 1) * 128],
                                               hid[:, kk, n0:n0 + nsz], start=(kk == 0), stop=(kk == 3))
                    return ins
                pg.add("pe", emit, r=[("wbf", sd[0]), ("wbf", sd[1])] + [("ar", "hid", kk, n) for kk in range(4)],
                       w=[("ps", b)])
                pg.add("dve", (lambda m=m, b=b, n0=n0, nsz=nsz: nc.vector.scalar_tensor_tensor(
                    x[:, m, n0:n0 + nsz], ps[:, b, 0:nsz], 0.5, x[:, m, n0:n0 + nsz], ALU.mult, ALU.add)),
                    r=[("ps", b), ("x", m, n)], w=[("x", m, n)])
                if last and m == KT - 1 and n >= 1:
                    tail(n - 1)
            if last:
                tail(len(NPIECES) - 1)

    def zcols(self, ct, n):
        raise NotImplementedError

    def zview(self, ct, n, sh=2):
        z = self.z
        n0, nsz = NPIECES[n]
        if n < 4:
            return z[:, ct, sh + n0: sh + n0 + nsz]
        v = z[:, ct, 2 + NP: NTZ].rearrange("p (s t) -> p s t", t=LS + 2)
        return v[:, :, sh:sh + LS]

    def pview(self, ap2d, n):
        if n < 4:
            return ap2d
        return ap2d.rearrange("p (s t) -> p s t", t=LS)

    def stage_conv(self):
        nc = self.nc
        pg = self.pg
        x, xn, z, ps = self.x, self.xn, self.z, self.ps
        convw = self.convw
        self.arena_fence(extra=[("dm",)])
        for ct in range(KT):
            pg.add("pool", (lambda ct=ct: nc.gpsimd.memset(z[:, ct, 0:2], 0.0)), w=[("ar", "z", ct, "h")])
        self.dma(self.cachef[:].rearrange("p a b c -> p (a b c)"), self.d_cache, w=[("cachef",)])
        for ct in range(KT):
            def emit(ct=ct):
                v = z[:, ct, 2 + NP: NTZ].rearrange("p (s t) -> p s t", t=LS + 2)
                return nc.vector.tensor_copy(v[:, :, 0:2], self.cachef[:, ct, :, :])
            pg.add("dve", emit, r=[("cachef",)], w=[("ar", "z", ct, "hs")])
        win = self.d_win
        for j in range(KT):
            s = self.load_piece(self.pair_piece_ap(win, D + j * 128, 2 * D + j * 128), self.pair_view)
            for n, (n0, nsz) in enumerate(NPIECES):
                bgc, bv = self.pair_matmuls(
                    s, lambda k, n0, nsz: xn[:, k, n0:n0 + nsz],
                    lambda n: [("xn", k, n) for k in range(KT)], n, n0, nsz)
                t = self.tslot()
                tm = self.tmp[t]
                pg.add("act", (lambda bv=bv, tm=tm, nsz=nsz: nc.scalar.activation(
                    tm[:, 0:nsz], ps[:, bv, 0:nsz], AF.Copy)), r=[("ps", bv)], w=[("tmp", t)])
                pg.add("dve", (lambda bgc=bgc, tm=tm, j=j, n=n, nsz=nsz: nc.vector.tensor_tensor(
                    self.zview(j, n), self.pview(ps[:, bgc, 0:nsz], n), self.pview(tm[:, 0:nsz], n), ALU.mult)),
                    r=[("ps", bgc), ("tmp", t)], w=[("ar", "z", j, n)])
                if n == 3:
                    pg.add("dve", (lambda bgc=bgc, tm=tm, j=j: nc.vector.tensor_tensor(
                        self.zl_p[:, j, :], ps[:, bgc, 510:512], tm[:, 510:512], ALU.mult)),
                        r=[("ps", bgc), ("tmp", t)], w=[("zl_p", j)])
                if n == 4:
                    pg.add("dve", (lambda bgc=bgc, tm=tm, j=j: nc.vector.tensor_tensor(
                        self.zl_s[:, j, :, :], self.pview(ps[:, bgc, 0:128], 4)[:, :, 6:8],
                        self.pview(tm[:, 0:128], 4)[:, :, 6:8], ALU.mult)),
                        r=[("ps", bgc), ("tmp", t)], w=[("zl_s", j)])
        idb = self.identb
        for jp in range(KT // 2):
            s = self.load_piece(self.pair_piece_ap(win, (2 * jp) * 128, (2 * jp + 1) * 128), self.pair_view)
            for h in range(2):
                j = 2 * jp + h
                cd = self.cdiag[j % 2]
                cdk = ("dm", "cd", j % 2)
                for kk in range(3):
                    pg.add("dve", (lambda kk=kk, j=j, cd=cd: nc.vector.tensor_scalar(
                        cd[:, kk, :], idb[:], convw[:, kk, j:j + 1], None, ALU.mult)),
                        r=[("identb",), ("convw",)], w=[cdk + (kk,)])
                for n in (4, 3, 2, 1, 0):
                    n0, nsz = NPIECES[n]
                    b = self.bank()
                    wv = self.wbf[s][:, :].rearrange("p (h k c) -> p h k c", h=2, k=KT)

                    def emit(h=h, b=b, n0=n0, nsz=nsz, wv=wv):
                        ins = None
                        for k in range(KT):
                            ins = nc.tensor.matmul(ps[:, b, 0:nsz], wv[:, h, k, :], xn[:, k, n0:n0 + nsz],
                                                   start=(k == 0), stop=(k == KT - 1))
                        return ins
                    pg.add("pe", emit, r=[("wbf", s)] + [("xn", k, n) for k in range(KT)], w=[("ps", b)])
                    bc = self.bank()
                    zk = [("ar", "z", j, n), ("ar", "z", j, "h"), ("ar", "z", j, "hs")] + ([("ar", "z", j, n - 1)] if 0 < n < 4 else [])

                    def emitc(j=j, n=n, bc=bc, nsz=nsz, cd=cd):
                        ins = None
                        for kk in range(3):
                            ins = nc.tensor.matmul(self.pview(ps[:, bc, 0:nsz], n), cd[:, kk, :], self.zview(j, n, kk),
                                                   start=(kk == 0), stop=(kk == 2))
                        return ins
                    pg.add("pe", emitc, r=zk + [cdk], w=[("ps", bc)])
                    t = self.tslot()
                    tm = self.tmpb[t]
                    pg.add("act", (lambda bc=bc, tm=tm, nsz=nsz: nc.scalar.activation(
                        tm[:, 0:nsz], ps[:, bc, 0:nsz], AF.Copy)), r=[("ps", bc)], w=[("tmp", t)])
                    pg.add("dve", (lambda j=j, n=n, b=b, tm=tm, nsz=nsz: nc.vector.tensor_tensor(
                        self.zview(j, n), self.pview(ps[:, b, 0:nsz], n), self.pview(tm[:, 0:nsz], n), ALU.mult)),
                        r=[("ps", b), ("tmp", t)], w=[("ar", "z", j, n)])
        wout = self.d_wout.rearrange("(k p) n -> p k n", p=128)
        for mp in range(KT // 2):
            s = self.load_piece(wout[:, :, mp * 256:(mp + 1) * 256],
                                lambda a: a.rearrange("p (k c) -> p k c", k=KT))
            wv = self.wbf[s][:, :].rearrange("p (k c) -> p k c", k=KT)
            for h in range(2):
                m = 2 * mp + h
                for n, (n0, nsz) in enumerate(NPIECES):
                    b = self.bank()

                    def emit(h=h, b=b, n=n, nsz=nsz, wv=wv):
                        ins = None
                        for k in range(KT):
                            ins = nc.tensor.matmul(self.pview(ps[:, b, 0:nsz], n), wv[:, k, h * 128:(h + 1) * 128],
                                                   self.zview(k, n), start=(k == 0), stop=(k == KT - 1))
                        return ins
                    pg.add("pe", emit, r=[("wbf", s)] + [("ar", "z", k, n) for k in range(KT)], w=[("ps", b)])
                    pg.add("dve", (lambda m=m, b=b, n0=n0, nsz=nsz: nc.vector.tensor_tensor(
                        x[:, m, n0:n0 + nsz], ps[:, b, 0:nsz], x[:, m, n0:n0 + nsz], ALU.add)),
                        r=[("ps", b), ("x", m, n)], w=[("x", m, n)])
        self.dma(self.o_conv_p, self.zl_p[:].rearrange("p a b -> p (a b)"), r=[("zl_p",)])
        self.dma(self.o_conv_s, self.zl_s[:].rearrange("p a b c -> p (a b c)"), r=[("zl_s",)])
        self.arena_fence()

    def _small(self, eng, fn, r, w):
        self.pg.add(eng, fn, r=[("s5s", k) for k in r], w=[("s5s", k) for k in w])

    def reduce_angle(self, src, dst, eng="dve"):
        nc = self.nc
        T = self.s5s
        E = nc.vector
        self._small(eng, lambda: E.tensor_scalar(T["k"][:], T[src][:], 1.0 / TWO_PI, MAGIC, ALU.mult, ALU.add),
                    [src], ["k"])
        self._small(eng, lambda: E.tensor_scalar(T["k"][:], T["k"][:], MAGIC, -TWO_PI, ALU.subtract, ALU.mult),
                    ["k"], ["k"])
        self._small(eng, lambda: E.tensor_tensor(T[dst][:], T["k"][:], T[src][:], ALU.add), ["k", src], [dst])

    def stage_s5_consts(self):
        nc = self.nc
        pg = self.pg
        T = self.s5s
        V = nc.vector
        self.dma(T["lre"][:], self.d_lam_re, w=[("s5s", "lre")])
        self.dma(T["lim"][:], self.d_lam_im, w=[("s5s", "lim")])
        self.dma(T["dt"][:], self.d_ldt, w=[("s5s", "dt")])
        sm = self._small
        sm("act", lambda: nc.scalar.activation(T["dt"][:], T["dt"][:], AF.Exp), ["dt"], ["dt"])
        sm("dve", lambda: V.tensor_tensor(T["lr"][:], T["lre"][:], T["dt"][:], ALU.mult), ["lre", "dt"], ["lr"])
        sm("act", lambda: nc.scalar.activation(T["mag"][:], T["lr"][:], AF.Exp), ["lr"], ["mag"])
        sm("dve", lambda: V.tensor_tensor(T["th"][:], T["lim"][:], T["dt"][:], ALU.mult), ["lim", "dt"], ["th"])
        self.reduce_angle("th", "u")
        sm("dve", lambda: V.tensor_scalar(T["thn"][:], T["u"][:], 1.0 / TWO_PI, None, ALU.mult), ["u"], ["thn"])
        sm("act", lambda: nc.scalar.activation(T["sn"][:], T["u"][:], AF.Sin), ["u"], ["sn"])
        sm("act", lambda: nc.scalar.activation(T["au"][:], T["u"][:], AF.Abs), ["u"], ["au"])
        hp = self.halfpi
        pg.add("act", lambda: nc.scalar.activation(T["cs"][:], T["au"][:], AF.Sin, bias=hp[:], scale=-1.0),
               r=[("s5s", "au"), ("halfpi",)], w=[("s5s", "cs")])
        sm("dve", lambda: V.tensor_tensor(T["lbre"][:], T["mag"][:], T["cs"][:], ALU.mult), ["mag", "cs"], ["lbre"])
        sm("dve", lambda: V.tensor_tensor(T["lbim"][:], T["mag"][:], T["sn"][:], ALU.mult), ["mag", "sn"], ["lbim"])
        sm("dve", lambda: V.tensor_scalar(T["nr"][:], T["lbre"][:], -1.0, None, ALU.add), ["lbre"], ["nr"])
        sm("dve", lambda: V.tensor_tensor(T["den"][:], T["lre"][:], T["lre"][:], ALU.mult), ["lre"], ["den"])
        sm("dve", lambda: V.tensor_tensor(T["t1"][:], T["lim"][:], T["lim"][:], ALU.mult), ["lim"], ["t1"])
        sm("dve", lambda: V.tensor_tensor(T["den"][:], T["den"][:], T["t1"][:], ALU.add), ["den", "t1"], ["den"])
        sm("dve", lambda: V.reciprocal(T["rden"][:], T["den"][:]), ["den"], ["rden"])
        sm("dve", lambda: V.tensor_tensor(T["t1"][:], T["nr"][:], T["lre"][:], ALU.mult), ["nr", "lre"], ["t1"])
        sm("dve", lambda: V.tensor_tensor(T["t2"][:], T["lbim"][:], T["lim"][:], ALU.mult), ["lbim", "lim"], ["t2"])
        sm("dve", lambda: V.tensor_tensor(T["t1"][:], T["t1"][:], T["t2"][:], ALU.add), ["t1", "t2"], ["t1"])
        sm("dve", lambda: V.tensor_tensor(T["fre"][:], T["t1"][:], T["rden"][:], ALU.mult), ["t1", "rden"], ["fre"])
        sm("dve", lambda: V.tensor_tensor(T["t1"][:], T["lbim"][:], T["lre"][:], ALU.mult), ["lbim", "lre"], ["t1"])
        sm("dve", lambda: V.tensor_tensor(T["t2"][:], T["nr"][:], T["lim"][:], ALU.mult), ["nr", "lim"], ["t2"])
        sm("dve", lambda: V.tensor_tensor(T["t1"][:], T["t1"][:], T["t2"][:], ALU.subtract), ["t1", "t2"], ["t1"])
        sm("dve", lambda: V.tensor_tensor(T["fim"][:], T["t1"][:], T["rden"][:], ALU.mult), ["t1", "rden"], ["fim"])
        sm("dve", lambda: V.tensor_tensor(T["t1"][:], T["fre"][:], T["fre"][:], ALU.mult), ["fre"], ["t1"])
        sm("dve", lambda: V.tensor_tensor(T["t2"][:], T["fim"][:], T["fim"][:], ALU.mult), ["fim"], ["t2"])
        sm("dve", lambda: V.tensor_tensor(T["t1"][:], T["t1"][:], T["t2"][:], ALU.add), ["t1", "t2"], ["t1"])
        sm("dve", lambda: V.reciprocal(T["t2"][:], T["t1"][:]), ["t1"], ["t2"])
        sm("dve", lambda: V.tensor_tensor(T["fire"][:], T["fre"][:], T["t2"][:], ALU.mult), ["fre", "t2"], ["fire"])
        sm("dve", lambda: V.scalar_tensor_tensor(T["fiim"][:], T["fim"][:], -1.0, T["t2"][:], ALU.mult, ALU.mult),
           ["fim", "t2"], ["fiim"])
        si = self.si
        self.dma(si["re"][:].rearrange("p a b -> p (a b)"), self.d_s0_re, w=[("si", "re")])
        self.dma(si["im"][:].rearrange("p a b -> p (a b)"), self.d_s0_im, w=[("si", "im")])

        def bc(name):
            return T[name][:].unsqueeze(2).to_broadcast([128, NPAIR, NSQ])
        self.cmul_inplace(si, "fire", "fiim")
        sm("dve", lambda: V.memset(T["cre"][:], 0.0), [], ["cre"])
        sm("dve", lambda: V.memset(T["cim"][:], 0.0), [], ["cim"])
        for nm, src in (("re", self.d_bt_re), ("im", self.d_bt_im)):
            ss = self.load_raw(src, lambda a: a, 1024)
            stg = self.stg[ss]
            dst = self.btc[nm]
            pg.add("pool", (lambda dst=dst, stg=stg: nc.gpsimd.tensor_copy(
                dst[:].rearrange("p a b -> p (a b)"), stg[:, 0:1024])), r=[("stg", ss)], w=[("btc", nm)])
        sl = [self.load_raw(src, lambda a: a, 1024) for src in (self.d_ct_re, self.d_ct_im)]
        for hf in range(2):
            qa = slice(16 * hf, 16 * hf + 16)
            cre = self.stg[sl[0]][:, 0:1024].rearrange("p (a b) -> p a b", b=32)[:, qa, :]
            cim = self.stg[sl[1]][:, 0:1024].rearrange("p (a b) -> p a b", b=32)[:, qa, :]

            def bc32(name, qa=qa):
                return T[name][:, qa].unsqueeze(2).to_broadcast([128, 16, 32])
            ta, tb = self.tslot(), self.tslot()
            t0 = self.tmp[ta][:, 0:512].rearrange("p (a b) -> p a b", b=32)
            t1 = self.tmp[tb][:, 0:512].rearrange("p (a b) -> p a b", b=32)
            rk = [("stg", sl[0]), ("stg", sl[1]), ("s5s", "fre"), ("s5s", "fim")]
            k0, k1 = ("tmp", ta), ("tmp", tb)
            pg.add("dve", (lambda t0=t0, cre=cre, bc32=bc32: V.tensor_tensor(t0, cre, bc32("fre"), ALU.mult)),
                   r=rk, w=[k0])
            pg.add("dve", (lambda t1=t1, cim=cim, bc32=bc32: V.tensor_tensor(t1, cim, bc32("fim"), ALU.mult)),
                   r=rk, w=[k1])
            pg.add("dve", (lambda t0=t0, t1=t1, qa=qa: V.tensor_tensor(self.cft["re"][:, qa, :], t0, t1, ALU.subtract)),
                   r=[k0, k1], w=[("cft", "re", hf)])
            pg.add("dve", (lambda t0=t0, cre=cre, bc32=bc32: V.tensor_tensor(t0, cre, bc32("fim"), ALU.mult)),
                   r=rk + [k0], w=[k0])
            pg.add("dve", (lambda t1=t1, cim=cim, bc32=bc32: V.tensor_tensor(t1, cim, bc32("fre"), ALU.mult)),
                   r=rk + [k1], w=[k1])
            pg.add("dve", (lambda t0=t0, t1=t1, qa=qa: V.scalar_tensor_tensor(
                self.cft["im"][:, qa, :], t0, -1.0, t1, ALU.mult, ALU.subtract)),
                r=[k0, k1], w=[("cft", "im", hf)])

    def cmul_inplace(self, S, fre, fim):
        nc = self.nc
        pg = self.pg
        V = nc.vector
        T = self.s5s

        def bc(name):
            return T[name][:].unsqueeze(2).to_broadcast([128, NPAIR, NSQ])
        ta, tb = self.tslot(), self.tslot()
        Ta = self.tmp[ta][:, 0:512].rearrange("p (a b) -> p a b", b=NSQ)
        Tb = self.tmp[tb][:, 0:512].rearrange("p (a b) -> p a b", b=NSQ)
        ka, kb = ("tmp", ta), ("tmp", tb)
        kf = [("s5s", fre), ("s5s", fim)]
        pg.add("dve", lambda: V.tensor_tensor(Ta, S["im"][:], bc(fim), ALU.mult), r=[("si", "im")] + kf, w=[ka])
        pg.add("dve", lambda: V.tensor_tensor(Tb, S["re"][:], bc(fim), ALU.mult), r=[("si", "re")] + kf, w=[kb])
        pg.add("dve", lambda: V.tensor_tensor(S["re"][:], S["re"][:], bc(fre), ALU.mult), r=[("si", "re"), kb] + kf,
               w=[("si", "re")])
        pg.add("dve", lambda: V.tensor_tensor(S["re"][:], S["re"][:], Ta, ALU.subtract), r=[("si", "re"), ka],
               w=[("si", "re")])
        pg.add("dve", lambda: V.tensor_tensor(S["im"][:], S["im"][:], bc(fre), ALU.mult), r=[("si", "im"), ka] + kf,
               w=[("si", "im")])
        pg.add("dve", lambda: V.tensor_tensor(S["im"][:], S["im"][:], Tb, ALU.add), r=[("si", "im"), kb],
               w=[("si", "im")])

    def stage_s5(self):
        nc = self.nc
        pg = self.pg
        V, G, A = nc.vector, nc.gpsimd, nc.scalar
        T = self.s5s
        x, xn, ps = self.x, self.xn, self.ps
        yv = self.rstd
        iota1 = self.iota1()
        hp = self.halfpi
        W = 512
        self.arena_fence()
        for ct in range(KT):
            pg.add("pool", (lambda ct=ct: G.tensor_scalar(self.dm[:, ct, :], self.ident(), self.dvec[:, ct:ct + 1], None, ALU.mult)),
                   r=[("cst",), ("dvec",)], w=[("dm", ct)])

        def tabkey(tp, n):
            return ("stg", tp, "tab", n)

        def pk(st, n):
            return ("ar", "s5", st, n)

        def gen_tables_parts(q, tp):
            tb = self.s5tab[tp]
            qs = slice(q, q + 1)
            ang = tb["ang"]
            ka = tabkey(tp, "ang")

            def p0():
                pg.add("dve", (lambda: V.tensor_scalar(ang[:], iota1, T["thn"][:, qs], MAGIC, ALU.mult, ALU.add)),
                       r=[("cst",), ("s5s", "thn")], w=[ka])
                pg.add("dve", (lambda: V.tensor_scalar(ang[:], ang[:], MAGIC, -TWO_PI, ALU.subtract, ALU.mult)),
                       r=[ka], w=[ka])
                pg.add("dve", (lambda: V.scalar_tensor_tensor(ang[:], iota1, T["u"][:, qs], ang[:], ALU.mult, ALU.add)),
                       r=[("cst",), ("s5s", "u"), ka], w=[ka])
                pg.add("dve", (lambda: V.tensor_scalar(ang[:], ang[:], math.pi, -math.pi, ALU.min, ALU.max)),
                       r=[ka], w=[ka])

            def p1():
                pg.add("act", (lambda: A.activation(tb["sin32"][:], ang[:], AF.Sin)), r=[ka], w=[tabkey(tp, "sin32")])
                pg.add("act", (lambda: A.activation(tb["sin16"][:], ang[:], AF.Sin)), r=[ka], w=[tabkey(tp, "sin16")])

            def p2():
                pg.add("act", (lambda: A.activation(tb["nsl"][:, 0:1], ang[:, W - 1:W], AF.Sin, scale=-1.0)), r=[ka],
                       w=[tabkey(tp, "nsl")])
                pg.add("act", (lambda: A.activation(tb["nsin16"][:], ang[:], AF.Sin, scale=-1.0)), r=[ka],
                       w=[("wbf", tp, "nsin")])

            def p3():
                pg.add("act", (lambda: A.activation(ang[:], ang[:], AF.Abs)), r=[ka], w=[ka])
                pg.add("act", (lambda: A.activation(tb["cos32"][:], ang[:], AF.Sin, bias=hp[:], scale=-1.0)),
                       r=[ka, ("halfpi",)], w=[tabkey(tp, "cos32")])

            def p4():
                pg.add("act", (lambda: A.activation(tb["cos16"][:], ang[:], AF.Sin, bias=hp[:], scale=-1.0)),
                       r=[ka, ("halfpi",)], w=[tabkey(tp, "cos16")])
            return [p0, p1, p2, p3, p4]

        def gen_pair_consts(q, ct, qi, pb):
            qs = slice(q, q + 1)
            for nm in ("re", "im"):
                bp = self.bpad[nm][pb]
                src = self.btc[nm]
                pg.add("dve", (lambda bp=bp, src=src: V.tensor_scalar(
                    bp[:], src[:, ct, :], self.rowmask(qi), None, ALU.mult)),
                    r=[("btc", nm), ("cst",)], w=[("bpad", nm, pb)])
            mm = self.magmask[pb]
            pg.add("dve", (lambda: V.tensor_scalar(mm[:], self.mask8(), T["mag"][:, qs], None, ALU.mult)),
                   r=[("cst",), ("s5s", "mag")], w=[("magmask", pb)])

        pieces = []
        pairno = 0
        for ct in range(KT):
            for qi in range(4):
                q = 4 * ct + qi
                for pc in range(5):
                    pieces.append(dict(ct=ct, qi=qi, q=q, pc=pc, tp=pairno % 2, first=(pc == 0),
                                       last_of_ct=(qi == 3 and pc == 4)))
                pairno += 1
        for i, pcd in enumerate(pieces):
            pcd["st"] = i % 2
            pcd["smp"] = (pcd["pc"] == 4)
            pcd["c0"] = NP if pcd["smp"] else pcd["pc"] * W
            pcd["csz"] = NS if pcd["smp"] else W
            pcd["n"] = pcd["pc"]

        def views(pcd):
            smp, csz = pcd["smp"], pcd["csz"]
            tb = self.s5tab[pcd["tp"]]
            S = self.s5p[pcd["st"]]
            if smp:
                def v3(a):
                    return a.rearrange("p (s t) -> p s t", t=LS)

                def tabv(t):
                    a = t[:, 0:LS]
                    return bass.AP(a.tensor, a.offset, [list(a.ap[0]), [0, NSQ], [1, LS]])
            else:
                def v3(a):
                    return a

                def tabv(t):
                    return t[:, 0:csz]
            return tb, S, v3, tabv

        def P1a(pcd):
            ct, q, qi, st, tp, smp, c0, csz, n = (pcd[k] for k in ("ct", "q", "qi", "st", "tp", "smp", "c0", "csz", "n"))
            pb = tp
            tb, S, v3, tabv = views(pcd)
            bre, bim = self.bank(), self.bank()
            for nm, b0 in (("re", bre), ("im", bim)):
                bp = self.bpad[nm][pb]
                pg.add("pe", (lambda bp=bp, b0=b0: nc.tensor.matmul(ps[:, b0, 0:csz], bp[:], xn[:, ct, c0:c0 + csz],
                                                                   start=True, stop=True)),
                       r=[("bpad", nm, pb), ("xn", ct, n)], w=[("ps", b0)])
            pg.add("act", (lambda: A.activation(S["pr"][:, 0:csz], ps[:, bre, 0:csz], AF.Copy)), r=[("ps", bre)], w=[pk(st, "pr")])
            pg.add("act", (lambda: A.activation(S["pi"][:, 0:csz], ps[:, bim, 0:csz], AF.Copy)), r=[("ps", bim)], w=[pk(st, "pi")])

        def P1b(pcd):
            ct, q, qi, st, tp, smp, c0, csz, n = (pcd[k] for k in ("ct", "q", "qi", "st", "tp", "smp", "c0", "csz", "n"))
            tb, S, v3, tabv = views(pcd)
            pr, pi, q1, q2, q3, q4 = (v3(S[k][:, 0:csz]) for k in ("pr", "pi", "q1", "q2", "q3", "q4"))
            cw, sw = tabv(tb["cos16"]), tabv(tb["sin16"])
            kc, ks = tabkey(tp, "cos16"), tabkey(tp, "sin16")
            if smp:
                pg.add("dve", (lambda: V.tensor_tensor(q1, pr, cw, ALU.mult)), r=[pk(st, "pr"), kc], w=[pk(st, "q1")])
                pg.add("dve", (lambda: V.tensor_tensor(q2, pi, sw, ALU.mult)), r=[pk(st, "pi"), ks], w=[pk(st, "q2")])
                pg.add("dve", (lambda: V.tensor_tensor(q3, pi, cw, ALU.mult)), r=[pk(st, "pi"), kc], w=[pk(st, "q3")])
                pg.add("dve", (lambda: V.tensor_tensor(q4, pr, sw, ALU.mult)), r=[pk(st, "pr"), ks], w=[pk(st, "q4")])
            if not smp:
                nsw = tabv(tb["nsin16"])
                kn = ("wbf", tp, "nsin")
                pg.add("dve", (lambda: V.tensor_tensor(q1, pr, cw, ALU.mult)), r=[pk(st, "pr"), kc], w=[pk(st, "q1")])
                pg.add("dve", (lambda: V.tensor_tensor(q2, pi, sw, ALU.mult)), r=[pk(st, "pi"), ks], w=[pk(st, "q2")])
                pg.add("dve", (lambda: V.tensor_tensor(q3, pi, cw, ALU.mult)), r=[pk(st, "pi"), kc], w=[pk(st, "q3")])
                pg.add("dve", (lambda: V.tensor_tensor(q4, pr, nsw, ALU.mult)), r=[pk(st, "pr"), kn], w=[pk(st, "q4")])
                sre, sim_ = self.bank(), self.bank()
                pcd["sbanks"] = (sre, sim_)
                idb = self.identb
                for bnk, ka, kb in ((sre, "q1", "q2"), (sim_, "q3", "q4")):
                    def emits(bnk=bnk, ka=ka, kb=kb):
                        nc.tensor.matmul(ps[:, bnk, 0:csz], idb[:], S[ka][:, 0:csz], start=True, stop=False)
                        return nc.tensor.matmul(ps[:, bnk, 0:csz], idb[:], S[kb][:, 0:csz], start=False, stop=True)
                    pg.add("pe", emits, r=[pk(st, ka), pk(st, kb), ("identb",)], w=[("ps", bnk)])
                return
            if smp:
                f1, f3 = v3(S["f1"][:, 0:csz]), v3(S["f3"][:, 0:csz])
                pg.add("dve", (lambda: V.tensor_tensor(f1, q1, q2, ALU.add)), r=[pk(st, "q1"), pk(st, "q2")], w=[pk(st, "A2")])
                pg.add("dve", (lambda: V.tensor_tensor(f3, q3, q4, ALU.subtract)), r=[pk(st, "q3"), pk(st, "q4")], w=[pk(st, "A4")])
            else:
                pg.add("dve", (lambda: V.tensor_tensor(q1, q1, q2, ALU.add)), r=[pk(st, "q1"), pk(st, "q2")], w=[pk(st, "q1")])
                pg.add("dve", (lambda: V.tensor_tensor(q3, q3, q4, ALU.subtract)), r=[pk(st, "q3"), pk(st, "q4")], w=[pk(st, "q3")])

        def P2(pcd):
            ct, q, qi, st, tp, smp, c0, csz, n = (pcd[k] for k in ("ct", "q", "qi", "st", "tp", "smp", "c0", "csz", "n"))
            pb = tp
            qs = slice(q, q + 1)
            tb, S, v3, tabv = views(pcd)
            A1, A2, A3, A4 = S["q1"], S["A2"], S["q3"], S["A4"]
            a1, a2, a3, a4 = (v3(S[k][:, 0:csz]) for k in ("q1", "A2", "q3", "A4"))
            kc, ks = tabkey(tp, "cos32"), tabkey(tp, "sin32")
            if smp:
                si = self.si
                mm = self.magmask[pb]
                F1, F3 = S["f1"], S["f3"]
                f1v, f3v = v3(F1[:, 0:csz]), v3(F3[:, 0:csz])
                for Av, nm, kk in ((f1v, "re", "A2"), (f3v, "im", "A4")):
                    pg.add("dve", (lambda Av=Av, nm=nm: V.scalar_tensor_tensor(
                        Av[:, :, 0], si[nm][:, q, :], T["mag"][:, qs], Av[:, :, 0], ALU.mult, ALU.add)),
                        r=[("si", nm, q), ("s5s", "mag"), pk(st, kk)], w=[pk(st, kk)])
                pg.add("dve", (lambda: V.tensor_tensor_scan(A2[:, 0:NS], mm[:], F1[:, 0:NS], 0.0, ALU.mult, ALU.add)),
                       r=[pk(st, "A2"), ("magmask", pb)], w=[pk(st, "A2")])
                pg.add("dve", (lambda: V.tensor_tensor_scan(A4[:, 0:NS], mm[:], F3[:, 0:NS], 0.0, ALU.mult, ALU.add)),
                       r=[pk(st, "A4"), ("magmask", pb)], w=[pk(st, "A4")])
            else:
                mb = T["mag"][:, qs].to_broadcast([128, csz])
                sre, sim_ = pcd["sbanks"]
                pg.add("dve", (lambda: V.tensor_tensor_scan(A2[:, 0:csz], mb, ps[:, sre, 0:csz], T["cre"][:, qs], ALU.mult, ALU.add)),
                       r=[("ps", sre), ("s5s", "mag"), ("s5s", "cre", q)], w=[pk(st, "A2")])
                pg.add("dve", (lambda: V.tensor_tensor_scan(A4[:, 0:csz], mb, ps[:, sim_, 0:csz], T["cim"][:, qs], ALU.mult, ALU.add)),
                       r=[("ps", sim_), ("s5s", "mag"), ("s5s", "cim", q)], w=[pk(st, "A4")])
            pg.add("act", (lambda: A.activation(S["g1"][:, 0:csz], A2[:, 0:csz], AF.Copy)), r=[pk(st, "A2")], w=[pk(st, "g1")])
            pg.add("act", (lambda: A.activation(S["g2"][:, 0:csz], A4[:, 0:csz], AF.Copy)), r=[pk(st, "A4")], w=[pk(st, "g2")])
            cy = self.cty[st]
            if smp:
                L = LS - 1
                cl, sl_ = tb["cos32"][:, L:L + 1], tb["sin32"][:, L:L + 1]
                t1 = cy[:, 0:1]
                sc = self.scr[st][:, 0:NSQ]
                pg.add("dve", (lambda: V.tensor_scalar(sc, a4[:, :, L], sl_, None, ALU.mult)),
                       r=[pk(st, "A4"), ks], w=[("scr", st, 0)])
                pg.add("dve", (lambda: V.scalar_tensor_tensor(self.si["re"][:, q, :], a2[:, :, L], cl, sc, ALU.mult, ALU.subtract)),
                       r=[pk(st, "A2"), kc, ("scr", st, 0)], w=[("si", "re", q)])
                sc2 = self.scr[st][:, NSQ:2 * NSQ]
                pg.add("dve", (lambda: V.tensor_scalar(sc2, a2[:, :, L], sl_, None, ALU.mult)),
                       r=[pk(st, "A2"), ks], w=[("scr", st, 1)])
                pg.add("dve", (lambda: V.scalar_tensor_tensor(self.si["im"][:, q, :], a4[:, :, L], cl, sc2, ALU.mult, ALU.add)),
                       r=[pk(st, "A4"), kc, ("scr", st, 1)], w=[("si", "im", q)])
            else:
                L = csz - 1
                cl, sl_ = tb["cos32"][:, L:L + 1], tb["sin32"][:, L:L + 1]
                t1, t2 = cy[:, 0:1], cy[:, 1:2]
                nsl = tb["nsl"][:, 0:1]
                pg.add("act", (lambda: A.activation(t1, A4[:, L:L + 1], AF.Copy, scale=nsl)),
                       r=[pk(st, "A4"), tabkey(tp, "nsl")], w=[("cty", st, 0)])
                pg.add("act", (lambda: A.activation(T["cre"][:, qs], A2[:, L:L + 1], AF.Identity, bias=t1, scale=cl)),
                       r=[pk(st, "A2"), kc, ("cty", st, 0)], w=[("s5s", "cre", q)])
                pg.add("act", (lambda: A.activation(t2, A2[:, L:L + 1], AF.Copy, scale=sl_)),
                       r=[pk(st, "A2"), ks], w=[("cty", st, 1)])
                pg.add("act", (lambda: A.activation(T["cim"][:, qs], A4[:, L:L + 1], AF.Identity, bias=t2, scale=cl)),
                       r=[pk(st, "A4"), kc, ("cty", st, 1)], w=[("s5s", "cim", q)])

        def P3(pcd):
            ct, q, qi, st, tp, smp, c0, csz, n = (pcd[k] for k in ("ct", "q", "qi", "st", "tp", "smp", "c0", "csz", "n"))
            tb, S, v3, tabv = views(pcd)
            g1, g2, m1, m2, m3, m4 = (v3(S[k][:, 0:csz]) for k in ("g1", "g2", "m1", "m2", "m3", "m4"))
            cw, sw, nsw = tabv(tb["cos16"]), tabv(tb["sin16"]), tabv(tb["nsin16"])
            kc, ks, kn = tabkey(tp, "cos16"), tabkey(tp, "sin16"), ("wbf", tp, "nsin")
            pg.add("dve", (lambda: V.tensor_tensor(m1, g1, cw, ALU.mult)), r=[pk(st, "g1"), kc], w=[pk(st, "m1")])
            pg.add("dve", (lambda: V.tensor_tensor(m2, g2, nsw, ALU.mult)), r=[pk(st, "g2"), kn], w=[pk(st, "m2")])
            pg.add("dve", (lambda: V.tensor_tensor(m3, g1, sw, ALU.mult)), r=[pk(st, "g1"), ks], w=[pk(st, "m3")])
            pg.add("dve", (lambda: V.tensor_tensor(m4, g2, cw, ALU.mult)), r=[pk(st, "g2"), kc], w=[pk(st, "m4")])
            by = self.bank()
            r0 = 32 * qi

            def emitc():
                o = ps[r0:r0 + 32, by, 0:csz]
                tp_ = (0, r0)
                nc.tensor.matmul(o, self.cft["re"][:, q, :], S["m1"][:, 0:csz], start=True, stop=False, tile_position=tp_)
                nc.tensor.matmul(o, self.cft["re"][:, q, :], S["m2"][:, 0:csz], start=False, stop=False, tile_position=tp_)
                nc.tensor.matmul(o, self.cft["im"][:, q, :], S["m3"][:, 0:csz], start=False, stop=False, tile_position=tp_)
                nc.tensor.matmul(o, self.cft["im"][:, q, :], S["m4"][:, 0:csz], start=False, stop=False, tile_position=tp_)
                return nc.tensor.matmul(o, self.dm[:, ct, r0:r0 + 32], xn[:, ct, c0:c0 + csz],
                                        start=False, stop=True, tile_position=tp_)
            pg.add("pe", emitc, r=[pk(st, "m1"), pk(st, "m2"), pk(st, "m3"), pk(st, "m4"), ("cft", "re"), ("cft", "im"),
                                   ("dm", ct), ("xn", ct, n)], w=[("ps", by)])
            pg.add("act", (lambda: A.activation(yv[r0:r0 + 32, c0:c0 + csz], ps[r0:r0 + 32, by, 0:csz], AF.Copy)),
                   r=[("ps", by)], w=[("rstd", n, qi)])
            if pcd["last_of_ct"]:
                gelu(ct)

        def gelu(ct):
            for n, (n0, nsz) in enumerate(NPIECES):
                t = self.tslot()
                tm = self.tmp[t]
                t2 = self.tslot()
                tm2 = self.tmp[t2]
                yk = [("rstd", n, qi) for qi in range(4)]
                ysl = yv[:, n0:n0 + nsz]
                pg.add("act", (lambda tm=tm, ysl=ysl, nsz=nsz: A.activation(tm[:, 0:nsz], ysl, AF.Square)),
                       r=yk, w=[("tmp", t)])
                pg.add("act", (lambda tm=tm, nsz=nsz: A.activation(tm[:, 0:nsz], tm[:, 0:nsz], AF.Identity, bias=1.0, scale=0.044715)),
                       r=[("tmp", t)], w=[("tmp", t)])
                pg.add("dve", (lambda tm=tm, ysl=ysl, nsz=nsz: V.tensor_tensor(tm[:, 0:nsz], tm[:, 0:nsz], ysl, ALU.mult)),
                       r=[("tmp", t)] + yk, w=[("tmp", t)])
                pg.add("act", (lambda tm=tm, tm2=tm2, nsz=nsz: A.activation(
                    tm2[:, 0:nsz], tm[:, 0:nsz], AF.Tanh, scale=math.sqrt(2.0 / math.pi))),
                    r=[("tmp", t)], w=[("tmp", t2)])
                pg.add("act", (lambda tm2=tm2, nsz=nsz: A.activation(tm2[:, 0:nsz], tm2[:, 0:nsz], AF.Identity, bias=0.5, scale=0.5)),
                       r=[("tmp", t2)], w=[("tmp", t2)])
                pg.add("dve", (lambda tm2=tm2, ysl=ysl, ct=ct, n0=n0, nsz=nsz: V.tensor_tensor(
                    xn[:, ct, n0:n0 + nsz], tm2[:, 0:nsz], ysl, ALU.mult)),
                    r=[("tmp", t2)] + yk, w=[("xn", ct, n)])

        NPC = len(pieces)
        p00 = pieces[0]
        for part in gen_tables_parts(p00["q"], p00["tp"]):
            part()
        gen_pair_consts(p00["q"], p00["ct"], p00["qi"], p00["tp"])
        P1a(pieces[0])
        next_parts = None
        for i in range(NPC + 2):
            if i + 1 < NPC:
                P1a(pieces[i + 1])
            if i < NPC:
                P1b(pieces[i])
            if i - 2 >= 0:
                P3(pieces[i - 2])
            if 0 <= i - 1 < NPC:
                P2(pieces[i - 1])
            if i < NPC:
                k = pieces[i]["pc"]
                nxt = i - k + 5
                if nxt < NPC:
                    if k == 0:
                        nf = pieces[nxt]
                        next_parts = gen_tables_parts(nf["q"], nf["tp"])
                    next_parts[k]()
                    if k == 3:
                        nf = pieces[nxt]
                        gen_pair_consts(nf["q"], nf["ct"], nf["qi"], nf["tp"])
        self.arena_fence(extra=[("stg", 0), ("stg", 1), ("wbf", 0), ("wbf", 1)])
        self.cmul_inplace(self.si, "fre", "fim")
        self.dma(self.o_sre_s, self.si["re"][:].rearrange("p a b -> p (a b)"), r=[("si", "re")])
        self.dma(self.o_sim_s, self.si["im"][:].rearrange("p a b -> p (a b)"), r=[("si", "im")])
        sm = self._small
        sm("dve", lambda: V.tensor_tensor(T["ore"][:], T["cre"][:], T["fre"][:], ALU.mult), ["cre", "fre"], ["ore"])
        sm("dve", lambda: V.tensor_tensor(T["t1"][:], T["cim"][:], T["fim"][:], ALU.mult), ["cim", "fim"], ["t1"])
        sm("dve", lambda: V.tensor_tensor(T["ore"][:], T["ore"][:], T["t1"][:], ALU.subtract), ["ore", "t1"], ["ore"])
        sm("dve", lambda: V.tensor_tensor(T["oim"][:], T["cre"][:], T["fim"][:], ALU.mult), ["cre", "fim"], ["oim"])
        sm("dve", lambda: V.tensor_tensor(T["t1"][:], T["cim"][:], T["fre"][:], ALU.mult), ["cim", "fre", "ore"], ["t1"])
        sm("dve", lambda: V.tensor_tensor(T["oim"][:], T["oim"][:], T["t1"][:], ALU.add), ["oim", "t1"], ["oim"])
        self.dma(self.o_sre_p, T["ore"][:], r=[("s5s", "ore")])
        self.dma(self.o_sim_p, T["oim"][:], r=[("s5s", "oim")])
        wglu = self.d_wglu
        for j in range(KT):
            s = self.load_piece(self.pair_piece_ap(wglu, j * 128, D + j * 128), self.pair_view)
            for n, (n0, nsz) in enumerate(NPIECES):
                ba, bb = self.pair_matmuls(
                    s, lambda k, n0, nsz: xn[:, k, n0:n0 + nsz],
                    lambda n: [("xn", k, n) for k in range(KT)], n, n0, nsz)
                t = self.tslot()
                tm = self.tmp[t]
                pg.add("act", (lambda bb=bb, tm=tm, nsz=nsz: nc.scalar.activation(
                    tm[:, 0:nsz], ps[:, bb, 0:nsz], AF.Sigmoid)), r=[("ps", bb)], w=[("tmp", t)])
                pg.add("dve", (lambda ba=ba, tm=tm, nsz=nsz: V.tensor_tensor(
                    tm[:, 0:nsz], ps[:, ba, 0:nsz], tm[:, 0:nsz], ALU.mult)),
                    r=[("ps", ba), ("tmp", t)], w=[("tmp", t)])
                pg.add("dve", (lambda j=j, tm=tm, n0=n0, nsz=nsz: V.tensor_tensor(
                    x[:, j, n0:n0 + nsz], x[:, j, n0:n0 + nsz], tm[:, 0:nsz], ALU.add)),
                    r=[("tmp", t), ("x", j, n)], w=[("x", j, n)])

    def stage_store(self):
        yT = self.o_yT.rearrange("(k p) n -> p k n", p=128)
        for n, (n0, nsz) in enumerate(NPIECES):
            self.dma(yT[:, :, n0:n0 + nsz], self.x[:, :, n0:n0 + nsz], r=[("x", ct, n) for ct in range(KT)])

    def build(self):
        st = self.stages
        full = "all" in st
        self.stage_load()
        nrm = lambda gi, final=False: (lambda n: self.norm_piece(gi, n, final))
        self.stage_norm(0)
        if full or "s5" in st:
            self.stage_s5_consts()
        if full:
            self.stage_ffn(0, tail=nrm(1))
            self.stage_s5()
            self.stage_norm(2)
            self.stage_ffn(1, tail=nrm(3))
            self.stage_ffn(2, tail=nrm(4))
            self.stage_conv()
            self.stage_norm(5)
            self.stage_ffn(3, tail=nrm(6, True))
        else:
            if "ffn" in st:
                self.stage_ffn(0, tail=nrm(6, True))
            if "s5" in st:
                if "ffn" not in st:
                    pass
                self.stage_norm(1)
                self.stage_s5()
                self.stage_norm(6, final=True)
            if "conv" in st:
                self.stage_norm(4)
                self.stage_conv()
                self.stage_norm(6, final=True)
        self.stage_store()
        self.pg.finalize()
        return self.nc


def _consts():
    c = np.zeros((128, 512 + 128 + 4 + 128), np.float32)
    c[:, 0:512] = np.arange(1, 513, dtype=np.float32)[None, :]
    m8 = np.ones((NSQ, LS), np.float32)
    m8[:, 0] = 0.0
    c[:, 512:640] = m8.reshape(1, 128)
    for i in range(4):
        c[32 * i:32 * i + 32, 640 + i] = 1.0
    c[:, 644:772] = np.eye(128, dtype=np.float32)
    return c


def _chan_layout(v):
    v = np.asarray(v, np.float32)
    lead = v.shape[:-1]
    return np.ascontiguousarray(np.moveaxis(v.reshape(lead + (KT, 128)), -1, 0))


def _pair_layout(a):
    a = np.asarray(a, np.float32).reshape(NPAIR, 2, 64)
    return np.ascontiguousarray(a.transpose(1, 2, 0).reshape(128, NPAIR))


def _shared_inputs(inp):
    sh = {}
    sh["w_gu"] = np.ascontiguousarray(np.asarray(inp["ffn_w_gate_up"], np.float32).reshape(4, D, 2 * DFF))
    sh["w_dn"] = np.ascontiguousarray(np.asarray(inp["ffn_w_down"], np.float32).reshape(4, DFF, D))
    sh["w_glu"] = np.ascontiguousarray(np.asarray(inp["ssm_w_glu"], np.float32)[0])
    sh["w_in"] = np.ascontiguousarray(np.asarray(inp["conv_w_in"], np.float32)[0])
    sh["w_out"] = np.ascontiguousarray(np.asarray(inp["conv_w_out"], np.float32)[0])
    g = np.concatenate([np.asarray(inp["norm_g"], np.float32).reshape(6, D),
                        np.asarray(inp["final_norm_g"], np.float32).reshape(1, D)], axis=0)
    sh["gvec"] = _chan_layout(g).reshape(128, 7 * KT)
    sh["dvec"] = _chan_layout(np.asarray(inp["ssm_d"], np.float32)[0]).reshape(128, KT)
    sh["convw"] = _chan_layout(np.asarray(inp["conv_w"], np.float32)[0]).reshape(128, 3 * KT)
    sh["lamT_re"] = _pair_layout(inp["ssm_lam_re"][0])
    sh["lamT_im"] = _pair_layout(inp["ssm_lam_im"][0])
    ldt = np.repeat(np.asarray(inp["ssm_log_dt"], np.float32)[0][:, None], 64, axis=1)
    sh["ldtT"] = _pair_layout(ldt)
    for nm, key in (("btc_re", "ssm_b_re"), ("btc_im", "ssm_b_im")):
        B = np.asarray(inp[key], np.float32)[0]
        out = np.zeros((128, KT, 2, 64), np.float32)
        Bg = B.reshape(KT, 8, 64, 16)
        for gl in range(8):
            m = gl % 2
            out[16 * gl:16 * gl + 16, :, m, :] = Bg[:, gl, :, :].transpose(2, 0, 1)
        sh[nm] = out.reshape(128, KT * 128)
    for nm, key in (("ctc_re", "ssm_c_re"), ("ctc_im", "ssm_c_im")):
        C = np.asarray(inp[key], np.float32)[0].reshape(NPAIR, 2, 16, 64)
        out = np.zeros((2, 64, NPAIR, 2, 16), np.float32)
        for m in range(2):
            out[m, :, :, m, :] = C[:, m, :, :].transpose(2, 0, 1)
        sh[nm] = out.reshape(128, NPAIR * 32)
    sh["consts"] = _consts()
    return sh


def _core_inputs(inp, c):
    d = {}
    xp = np.asarray(inp["x_prompt"], np.float32)[c]
    xs = np.asarray(inp["x_sample"], np.float32)[NSQ * c:NSQ * (c + 1)].reshape(NS, D)
    d["xT"] = np.ascontiguousarray(np.concatenate([xp, xs], axis=0).T)
    cc = np.asarray(inp["cache_conv"], np.float32)[0, NSQ * c:NSQ * (c + 1)]
    d["cacheT"] = _chan_layout(cc).reshape(128, KT * NSQ * 2) if False else np.ascontiguousarray(
        np.moveaxis(cc.reshape(NSQ, 2, KT, 128), 3, 0).transpose(0, 3, 1, 2)).reshape(128, KT * NSQ * 2)
    for nm, key in (("s0_re", "state_ssm_re"), ("s0_im", "state_ssm_im")):
        s = np.asarray(inp[key], np.float32)[0, NSQ * c:NSQ * (c + 1)]
        s = s.reshape(NSQ, NPAIR, 2, 64).transpose(2, 3, 1, 0)
        d[nm] = np.ascontiguousarray(s).reshape(128, NPAIR * NSQ)
    return d


def _unpair(a):
    tail = a.shape[2:]
    a = a.reshape((2, 64, NPAIR) + tail)
    a = np.moveaxis(a, 2, 0)
    return a.reshape((64, 64) + tail)


_NC_CACHE = {}


def kernel(**inputs):
    if "full" not in _NC_CACHE:
        _NC_CACHE["full"] = Builder().build()
    nc = _NC_CACHE["full"]
    sh = _shared_inputs(inputs)
    in_maps = []
    for c in range(NCORES):
        m = dict(sh)
        m.update(_core_inputs(inputs, c))
        in_maps.append(m)
    res = run_bass_kernel_spmd(nc, in_maps, core_ids=list(range(NCORES)))
    R = res.results
    y_p = np.zeros((NCORES, NP, D), np.float32)
    y_s = np.zeros((NCORES * NSQ, LS, D), np.float32)
    sre_p = np.zeros((1, NCORES, 64, 64), np.float32)
    sim_p = np.zeros((1, NCORES, 64, 64), np.float32)
    conv_p = np.zeros((1, NCORES, 2, D), np.float32)
    sre_s = np.zeros((1, NCORES * NSQ, 64, 64), np.float32)
    sim_s = np.zeros((1, NCORES * NSQ, 64, 64), np.float32)
    conv_s = np.zeros((1, NCORES * NSQ, 2, D), np.float32)
    for c in range(NCORES):
        r = R[c]
        yT = np.asarray(r["yT"])
        y_p[c] = yT[:, :NP].T
        y_s[NSQ * c:NSQ * (c + 1)] = yT[:, NP:].T.reshape(NSQ, LS, D)
        sre_p[0, c] = _unpair(np.asarray(r["sre_p"]).reshape(128, NPAIR))
        sim_p[0, c] = _unpair(np.asarray(r["sim_p"]).reshape(128, NPAIR))
        sre_s[0, NSQ * c:NSQ * (c + 1)] = np.moveaxis(_unpair(np.asarray(r["sre_s"]).reshape(128, NPAIR, NSQ)), 2, 0)
        sim_s[0, NSQ * c:NSQ * (c + 1)] = np.moveaxis(_unpair(np.asarray(r["sim_s"]).reshape(128, NPAIR, NSQ)), 2, 0)
        cp = np.asarray(r["conv_p"]).reshape(128, KT, 2)
        conv_p[0, c] = cp.transpose(2, 1, 0).reshape(2, D)
        cs = np.asarray(r["conv_s"]).reshape(128, KT, NSQ, 2)
        conv_s[0, NSQ * c:NSQ * (c + 1)] = cs.transpose(2, 3, 1, 0).reshape(NSQ, 2, D)
    return (y_p, y_s, sre_p, sim_p, conv_p, sre_s, sim_s, conv_s)
```

```python
import math
import numpy as np
import concourse.bass as bass
import concourse.mybir as mybir
from concourse.bass_utils import run_bass_kernel_spmd

F32 = mybir.dt.float32
F32R = mybir.dt.float32r
BF16 = mybir.dt.bfloat16
AF = mybir.ActivationFunctionType
ALU = mybir.AluOpType

D = 1024
KT = 8
DFF = 4096
NP = 2048
NSQ = 16
LS = 8
NS = NSQ * LS
NT = NP + NS
NPAIR = 32
EPS = 1e-6
NCORES = 8
TWO_PI = 2.0 * math.pi
MAGIC = 12582912.0

NPIECES = [(0, 512), (512, 512), (1024, 512), (1536, 512), (2048, 128)]
NTZ = 2 + NP + NSQ * (2 + LS)
SPIECES = [(0, 1024), (1024, 1024), (2048, 128)]


class _Node:
    __slots__ = ("ch", "w", "r")

    def __init__(self):
        self.ch = {}
        self.w = None
        self.r = []


def _collect(node, ws, rs):
    if node.w is not None:
        ws.append(node.w)
    if node.r:
        rs.extend(node.r)
    for c in node.ch.values():
        _collect(c, ws, rs)


class Prog:
    ENGS = ("pe", "act", "dve", "pool", "sp")

    def __init__(self, nc, nds=16):
        self.nc = nc
        self.eng = {"pe": nc.tensor, "act": nc.scalar, "dve": nc.vector, "pool": nc.gpsimd, "sp": nc.sync}
        self.ops = []
        self.nds = nds

    def add(self, eng, fn, r=(), w=(), dma=False):
        self.ops.append((eng, fn, tuple(r), tuple(w), dma))

    def _touch(self, root, key, ws, rs):
        node = root
        for part in key:
            if node.w is not None:
                ws.append(node.w)
            if node.r:
                rs.extend(node.r)
            nxt = node.ch.get(part)
            if nxt is None:
                nxt = _Node()
                node.ch[part] = nxt
            node = nxt
        _collect(node, ws, rs)
        return node

    def finalize(self):
        import os
        mx = os.environ.get("K_MAXOPS")
        if mx:
            self.ops = self.ops[:int(mx)]
        nc = self.nc
        nds = self.nds
        sems = {e: nc.alloc_semaphore("sem_" + e) for e in self.ENGS}
        dsems = [nc.alloc_semaphore("dsem%d" % i) for i in range(nds)]
        seqc = {e: 0 for e in self.ENGS}
        known_e = {e: {e2: -1 for e2 in self.ENGS} for e in self.ENGS}
        known_d = {e: [0] * nds for e in self.ENGS}
        signaled = {e: set() for e in self.ENGS}
        root = _Node()
        plan = []
        ndma = 0
        for (eng, fn, r, w, dma) in self.ops:
            if dma:
                tok = ("d", ndma)
                ndma += 1
            else:
                tok = ("e", eng, seqc[eng])
                seqc[eng] += 1
            ws, rs = [], []
            rnodes = [self._touch(root, k, ws, rs) for k in r]
            deps = set(ws)
            ws2, rs2 = [], []
            wnodes = [self._touch(root, k, ws2, rs2) for k in w]
            deps.update(ws2)
            deps.update(rs2)
            if dma and tok[1] >= nds:
                deps.add(("d", tok[1] - nds))
            need_e, need_d = {}, {}
            for d in deps:
                if d[0] == "e":
                    if d[1] == eng and eng == "pe":
                        continue
                    if need_e.get(d[1], -1) < d[2]:
                        need_e[d[1]] = d[2]
                else:
                    s = d[1] % nds
                    val = 16 * (d[1] // nds + 1)
                    if need_d.get(s, 0) < val:
                        need_d[s] = val
            waits = []
            for e2, s2 in need_e.items():
                if known_e[eng][e2] < s2:
                    known_e[eng][e2] = s2
                    waits.append(("e", e2, s2))
                    signaled[e2].add(s2)
            for s, val in need_d.items():
                if known_d[eng][s] < val:
                    known_d[eng][s] = val
                    waits.append(("d", s, val))
            plan.append((waits, tok))
            for node in rnodes:
                if tok[0] == "e":
                    node.r = [t for t in node.r if not (t[0] == "e" and t[1] == tok[1])]
                node.r.append(tok)
            for node in wnodes:
                node.w = tok
                node.r = []
                node.ch = {}
        cnt = {}
        for e in self.ENGS:
            cnt[e] = {s: i + 1 for i, s in enumerate(sorted(signaled[e]))}
        self.n_sig = {e: len(cnt[e]) for e in self.ENGS}
        dcount = [0] * nds
        for (eng, fn, r, w, dma), (waits, tok) in zip(self.ops, plan):
            E = self.eng[eng]
            for wt in waits:
                if wt[0] == "e":
                    E.wait_ge(sems[wt[1]], cnt[wt[1]][wt[2]])
                else:
                    E.wait_ge(dsems[wt[1]], wt[2])
            ins = fn()
            if dma:
                s = tok[1] % nds
                ins.then_inc(dsems[s], 16)
                dcount[s] += 1
            elif tok[2] in cnt[eng]:
                ins.then_inc(sems[eng], 1)
        sp = self.eng["sp"]
        for s in range(nds):
            if dcount[s]:
                sp.wait_ge(dsems[s], 16 * dcount[s])


class Builder:
    def __init__(self, stages=("all",)):
        self.stages = stages
        nc = bass.Bass("TRN2", target_bir_lowering=False)
        self.nc = nc
        self.pg = Prog(nc)
        self._bank = 0
        self._wslot = 0
        self._sslot = 0
        self._tslot = 0
        self._declare_dram()
        self._alloc_sbuf()

    def _declare_dram(self):
        nc = self.nc

        def inp(name, shape):
            return nc.dram_tensor(name, list(shape), F32, kind="ExternalInput").ap()

        def outp(name, shape):
            return nc.dram_tensor(name, list(shape), F32, kind="ExternalOutput").ap()

        self.d_xT = inp("xT", (D, NT))
        self.d_wgu = inp("w_gu", (4, D, 2 * DFF))
        self.d_wdn = inp("w_dn", (4, DFF, D))
        self.d_wglu = inp("w_glu", (D, 2 * D))
        self.d_win = inp("w_in", (D, 3 * D))
        self.d_wout = inp("w_out", (D, D))
        self.d_gvec = inp("gvec", (128, 7 * KT))
        self.d_dvec = inp("dvec", (128, KT))
        self.d_convw = inp("convw", (128, 3 * KT))
        self.d_cache = inp("cacheT", (128, KT * NSQ * 2))
        self.d_lam_re = inp("lamT_re", (128, NPAIR))
        self.d_lam_im = inp("lamT_im", (128, NPAIR))
        self.d_ldt = inp("ldtT", (128, NPAIR))
        self.d_bt_re = inp("btc_re", (128, KT * 128))
        self.d_bt_im = inp("btc_im", (128, KT * 128))
        self.d_ct_re = inp("ctc_re", (128, NPAIR * 32))
        self.d_ct_im = inp("ctc_im", (128, NPAIR * 32))
        self.d_s0_re = inp("s0_re", (128, NPAIR * NSQ))
        self.d_s0_im = inp("s0_im", (128, NPAIR * NSQ))
        self.d_consts = inp("consts", (128, 512 + 128 + 4 + 128))
        self.o_yT = outp("yT", (D, NT))
        self.o_sre_p = outp("sre_p", (128, NPAIR))
        self.o_sim_p = outp("sim_p", (128, NPAIR))
        self.o_sre_s = outp("sre_s", (128, NPAIR * NSQ))
        self.o_sim_s = outp("sim_s", (128, NPAIR * NSQ))
        self.o_conv_p = outp("conv_p", (128, KT * 2))
        self.o_conv_s = outp("conv_s", (128, KT * NSQ * 2))

    def _alloc_sbuf(self):
        nc = self.nc
        self.x = nc.alloc_sbuf_tensor("sb_x", [128, KT, NT], F32)
        self.xn = nc.alloc_sbuf_tensor("sb_xn", [128, KT, NT], BF16)
        self.rstd = nc.alloc_sbuf_tensor("sb_rstd", [128, NT], F32)
        self.NSTG = 2
        self.NWBF = 4
        self.stg = [nc.alloc_sbuf_tensor("sb_stg%d" % i, [128, 2048], F32) for i in range(self.NSTG)]
        self.wbf = [nc.alloc_sbuf_tensor("sb_wbf%d" % i, [128, 2048], BF16) for i in range(self.NWBF)]
        self.tmp = [nc.alloc_sbuf_tensor("sb_tmp%d" % i, [128, 512], F32) for i in range(2)]
        self.tmpb = [nc.alloc_sbuf_tensor_at("sb_tmpb%d" % i, [128, 1024], BF16, offset=self._sb_offset(self.tmp[i]))
                     for i in range(2)]
        ARENA = KT * NTZ * 2
        base = nc.alloc_sbuf_tensor("sb_arena1", [128, ARENA // 4], F32)
        off0 = self._sb_offset(base)
        self.hid = nc.alloc_sbuf_tensor_at("sb_hid", [128, 4, NT], BF16, offset=off0)
        self.sq = nc.alloc_sbuf_tensor_at("sb_sq", [128, 2, NT], F32R, offset=off0 + 4 * NT * 2)
        self.z = nc.alloc_sbuf_tensor_at("sb_z", [128, KT, NTZ], BF16, offset=off0)
        self.s5p = []
        for st in range(2):
            o = off0 + st * 16384
            d = {}
            for i, n in enumerate(("A2", "A4")):
                d[n] = nc.alloc_sbuf_tensor_at("sb_s5_%s_%d" % (n, st), [128, 512], F32, offset=o + i * 2048)
            for i, n in enumerate(("pr", "pi", "q1", "q2", "q3", "q4", "g1", "g2", "m1", "m2", "m3", "m4")):
                d[n] = nc.alloc_sbuf_tensor_at("sb_s5_%s_%d" % (n, st), [128, 512], BF16, offset=o + 4096 + i * 1024)
            d["f1"] = nc.alloc_sbuf_tensor_at("sb_s5_f1_%d" % st, [128, NS], F32, offset=o + 1024)
            d["f3"] = nc.alloc_sbuf_tensor_at("sb_s5_f3_%d" % st, [128, NS], F32, offset=o + 2048 + 1024)
            self.s5p.append(d)
        self.s5tab = []
        for st in range(2):
            o = self._sb_offset(self.stg[st])
            d = {"cos32": nc.alloc_sbuf_tensor_at("sb_s5_cos32_%d" % st, [128, 512], F32, offset=o),
                 "sin32": nc.alloc_sbuf_tensor_at("sb_s5_sin32_%d" % st, [128, 512], F32, offset=o + 2048),
                 "cos16": nc.alloc_sbuf_tensor_at("sb_s5_cos16_%d" % st, [128, 512], BF16, offset=o + 4096),
                 "sin16": nc.alloc_sbuf_tensor_at("sb_s5_sin16_%d" % st, [128, 512], BF16, offset=o + 5120),
                 "ang": nc.alloc_sbuf_tensor_at("sb_s5_ang_%d" % st, [128, 512], F32, offset=o + 6144)}
            d["nsin16"] = nc.alloc_sbuf_tensor_at("sb_s5_nsin16_%d" % st, [128, 512], BF16,
                                                  offset=self._sb_offset(self.wbf[st]))
            d["nsl"] = nc.alloc_sbuf_tensor("sb_s5_nsl_%d" % st, [128, 2], F32)
            self.s5tab.append(d)
        self.gvec = nc.alloc_sbuf_tensor("sb_gvec", [128, 7, KT], F32)
        self.dvec = nc.alloc_sbuf_tensor("sb_dvec", [128, KT], F32)
        self.convw = nc.alloc_sbuf_tensor("sb_convw", [128, 3, KT], F32)
        self.cst = nc.alloc_sbuf_tensor("sb_cst", [128, 512 + 128 + 4 + 128], F32)
        self.onesr = nc.alloc_sbuf_tensor("sb_onesr", [128, 128], F32R)
        self.halfpi = nc.alloc_sbuf_tensor("sb_halfpi", [128, 1], F32)
        self.cachef = nc.alloc_sbuf_tensor("sb_cachef", [128, KT, NSQ, 2], F32)
        self.zl_p = nc.alloc_sbuf_tensor("sb_zl_p", [128, KT, 2], F32)
        self.zl_s = nc.alloc_sbuf_tensor("sb_zl_s", [128, KT, NSQ, 2], F32)
        self.s5s = {}
        for n in ["lre", "lim", "dt", "lr", "mag", "th", "thn", "k", "u", "au", "sn", "cs", "lbre", "lbim",
                  "nr", "den", "t1", "t2", "fre", "fim", "rden", "fire", "fiim", "cre", "cim", "ore", "oim"]:
            self.s5s[n] = nc.alloc_sbuf_tensor("sb_s5s_" + n, [128, NPAIR], F32)
        self.si = {n: nc.alloc_sbuf_tensor("sb_si_" + n, [128, NPAIR, NSQ], F32) for n in ("re", "im")}
        self.btc = {n: nc.alloc_sbuf_tensor("sb_btc_" + n, [128, KT, 128], BF16) for n in ("re", "im")}
        self.bpad = {n: [nc.alloc_sbuf_tensor("sb_bpad_%s%d" % (n, i), [128, 128], BF16) for i in range(2)]
                     for n in ("re", "im")}
        self.cft = {n: nc.alloc_sbuf_tensor("sb_cft_" + n, [128, NPAIR, 32], BF16) for n in ("re", "im")}
        self.magmask = [nc.alloc_sbuf_tensor("sb_magmask%d" % i, [128, 128], F32) for i in range(2)]
        self.dm = nc.alloc_sbuf_tensor("sb_dm", [128, KT, 128], BF16)
        self.identb = nc.alloc_sbuf_tensor("sb_identb", [128, 128], BF16)
        odm = self._sb_offset(self.dm)
        self.cdiag = [nc.alloc_sbuf_tensor_at("sb_cdiag%d" % i, [128, 3, 128], BF16, offset=odm + i * 768) for i in range(2)]
        self.cty = [nc.alloc_sbuf_tensor("sb_cty%d" % i, [128, 4], F32) for i in range(2)]
        self.scr = [nc.alloc_sbuf_tensor("sb_scr%d" % i, [128, 2 * NSQ], F32) for i in range(2)]
        self.fence_t = nc.alloc_sbuf_tensor("sb_fence", [128, 2], F32)
        self.ps = nc.alloc_psum_tensor("ps", [128, 8, 512], F32)

    def _sb_offset(self, handle):
        loc = self.nc.lookup_mloc(handle)
        for attr in ("offset", "addr", "address", "start", "byte_offset"):
            if hasattr(loc, attr):
                v = getattr(loc, attr)
                if isinstance(v, int):
                    return v
        raise RuntimeError("cannot find sbuf offset: %r %s" % (loc, dir(loc)))

    def arena_fence(self, extra=()):
        nc = self.nc
        ft = self.fence_t
        self.pg.add("dve", lambda: nc.vector.memset(ft[:], 0.0), w=[("ar",)] + list(extra))

    def bank(self, n=1):
        if n == 2 and self._bank % 2:
            self._bank += 1
        b = self._bank % 8
        self._bank += n
        return b

    def psk(self, b, n=1):
        return [("ps", b + i) for i in range(n)]

    def tslot(self):
        s = self._tslot % len(self.tmp)
        self._tslot += 1
        return s

    def dma(self, out, in_, r=(), w=()):
        nc = self.nc
        self.pg.add("sp", lambda: nc.sync.dma_start(out=out, in_=in_), r=r, w=w, dma=True)

    def load_raw(self, src_ap, view, nelem):
        ss = self._sslot % self.NSTG
        self._sslot += 1
        dst = view(self.stg[ss][:, 0:nelem])
        if isinstance(src_ap, (list, tuple)):
            for i, sa in enumerate(src_ap):
                self.dma(dst[:, i], sa, w=[("stg", ss, i)])
        else:
            self.dma(dst, src_ap, w=[("stg", ss, 0)])
        return ss

    def load_piece(self, src_ap, view, nelem=2048):
        nc = self.nc
        ss = self.load_raw(src_ap, view, nelem)
        s = self._wslot % self.NWBF
        self._wslot += 1
        stg = self.stg[ss]
        wbf = self.wbf[s]
        self.pg.add("pool", lambda: nc.gpsimd.tensor_copy(wbf[:, 0:nelem], stg[:, 0:nelem]),
                    r=[("stg", ss)], w=[("wbf", s)])
        return s

    def stage_load(self):
        nc = self.nc
        xT = self.d_xT.rearrange("(k p) n -> p k n", p=128)
        for n, (n0, nsz) in enumerate(NPIECES):
            self.dma(self.x[:, :, n0:n0 + nsz], xT[:, :, n0:n0 + nsz], w=[("x", ct, n) for ct in range(KT)])
        self.dma(self.gvec[:].rearrange("p a b -> p (a b)"), self.d_gvec, w=[("gvec",)])
        self.dma(self.dvec[:], self.d_dvec, w=[("dvec",)])
        self.dma(self.convw[:].rearrange("p a b -> p (a b)"), self.d_convw, w=[("convw",)])
        self.dma(self.cst[:], self.d_consts, w=[("cst",)])
        ones = self.onesr
        cst = self.cst
        t = self.tslot()
        tm = self.tmp[t]
        self.pg.add("dve", lambda: nc.vector.memset(tm[:, 0:128], 1.0), w=[("tmp", t)])
        self.pg.add("dve", lambda: nc.vector.tensor_copy(ones[:], tm[:, 0:128]), r=[("tmp", t)], w=[("ones",)])
        hp = self.halfpi
        self.pg.add("dve", lambda: nc.vector.memset(hp[:], math.pi / 2.0), w=[("halfpi",)])
        self.pg.add("pool", (lambda: nc.gpsimd.tensor_copy(self.identb[:], self.ident())), r=[("cst",)], w=[("identb",)])

    def iota1(self):
        return self.cst[:, 0:512]

    def mask8(self):
        return self.cst[:, 512:640]

    def rowmask(self, i):
        return self.cst[:, 640 + i:641 + i]

    def ident(self):
        return self.cst[:, 644:772]

    def norm_piece(self, gi, n, final=False):
        nc = self.nc
        pg = self.pg
        x, xn, sq, ones, rstd, gvec, ps = self.x, self.xn, self.sq, self.onesr, self.rstd, self.gvec, self.ps
        n0, nsz = NPIECES[n]
        b = self.bank()
        for ct in range(KT):
            sl = ct % 2
            pg.add("act", (lambda ct=ct, sl=sl: nc.scalar.activation(sq[:, sl, n0:n0 + nsz], x[:, ct, n0:n0 + nsz], AF.Square)),
                   r=[("x", ct, n)], w=[("ar", "sq", sl, n)])
            pg.add("pe", (lambda ct=ct, sl=sl: nc.tensor.matmul(
                ps[:, b, 0:nsz], ones[:], sq[:, sl, n0:n0 + nsz], start=(ct == 0), stop=(ct == KT - 1))),
                r=[("ar", "sq", sl, n), ("ones",)], w=[("ps", b)])
        pg.add("act", (lambda: nc.scalar.activation(
            rstd[:, n0:n0 + nsz], ps[:, b, 0:nsz], AF.Sqrt, bias=EPS, scale=1.0 / D)),
            r=[("ps", b)], w=[("rstd", n)])
        pg.add("dve", (lambda: nc.vector.reciprocal(rstd[:, n0:n0 + nsz], rstd[:, n0:n0 + nsz])),
               r=[("rstd", n)], w=[("rstd", n)])
        for ct in range(KT):
            dst = x if final else xn
            wk = ("x", ct, n) if final else ("xn", ct, n)
            pg.add("dve", (lambda ct=ct, dst=dst: nc.vector.scalar_tensor_tensor(
                dst[:, ct, n0:n0 + nsz], x[:, ct, n0:n0 + nsz], gvec[:, gi, ct:ct + 1],
                rstd[:, n0:n0 + nsz], ALU.mult, ALU.mult)),
                r=[("x", ct, n), ("rstd", n), ("gvec",)], w=[wk])

    def stage_norm(self, gi, final=False):
        for n in range(len(NPIECES)):
            self.norm_piece(gi, n, final)

    def pair_piece_ap(self, w2d, colA, colB):
        v = w2d.rearrange("(k p) n -> p k n", p=128)
        return [v[:, :, colA:colA + 128], v[:, :, colB:colB + 128]]

    @staticmethod
    def pair_view(stg_ap):
        return stg_ap.rearrange("p (h k c) -> p h k c", h=2, k=KT)

    def pair_matmuls(self, s, rhs_fn, rkeys_fn, n, n0, nsz):
        nc = self.nc
        ps = self.ps
        wv = self.wbf[s][:, :].rearrange("p (h k c) -> p h k c", h=2, k=KT)
        outb = []
        for h in range(2):
            b = self.bank()
            outb.append(b)

            def emit(h=h, b=b):
                ins = None
                for k in range(KT):
                    ins = nc.tensor.matmul(ps[:, b, 0:nsz], wv[:, h, k, :], rhs_fn(k, n0, nsz),
                                           start=(k == 0), stop=(k == KT - 1))
                return ins
            self.pg.add("pe", emit, r=[("wbf", s)] + rkeys_fn(n), w=[("ps", b)])
        return outb

    def stage_ffn(self, f, tail=None):
        nc = self.nc
        pg = self.pg
        x, xn, hid, ps = self.x, self.xn, self.hid, self.ps
        wgu = self.d_wgu[f]
        wdn = self.d_wdn[f].rearrange("(k p) n -> p k n", p=128)
        for c in range(DFF // 512):
            for j4 in range(4):
                j = c * 4 + j4
                s = self.load_piece(self.pair_piece_ap(wgu, j * 128, DFF + j * 128), self.pair_view)
                for n, (n0, nsz) in enumerate(NPIECES):
                    bg, bu = self.pair_matmuls(
                        s, lambda k, n0, nsz: xn[:, k, n0:n0 + nsz],
                        lambda n: [("xn", k, n) for k in range(KT)], n, n0, nsz)
                    t = self.tslot()
                    tm = self.tmp[t]
                    pg.add("act", (lambda bg=bg, tm=tm, nsz=nsz: nc.scalar.activation(
                        tm[:, 0:nsz], ps[:, bg, 0:nsz], AF.Silu)), r=[("ps", bg)], w=[("tmp", t)])
                    pg.add("dve", (lambda bu=bu, tm=tm, j4=j4, n0=n0, nsz=nsz: nc.vector.tensor_tensor(
                        hid[:, j4, n0:n0 + nsz], ps[:, bu, 0:nsz], tm[:, 0:nsz], ALU.mult)),
                        r=[("ps", bu), ("tmp", t)], w=[("ar", "hid", j4, n)])
            sd = []
            for i2 in range(2):
                kk0 = c * 4 + i2 * 2
                sd.append(self.load_piece(wdn[:, kk0:kk0 + 2, :],
                                          lambda a: a.rearrange("p (k c) -> p k c", k=2)))
            last = (c == DFF // 512 - 1) and tail is not None
            order = [(m, n) for n in range(len(NPIECES)) for m in range(KT)] if last else \
                    [(m, n) for m in range(KT) for n in range(len(NPIECES))]
            for (m, n) in order:
                n0, nsz = NPIECES[n]
                b = self.bank()

                def emit(m=m, b=b, n0=n0, nsz=nsz, sd=tuple(sd)):
                    ins = None
                    for kk in range(4):
                        wv = self.wbf[sd[kk // 2]][:, :].rearrange("p (k c) -> p k c", k=2)
                        ins = nc.tensor.matmul(ps[:, b, 0:nsz], wv[:, kk % 2, m * 128:(m + 1) * 128],
                                               hid[:, kk, n0:n0 + nsz], start=(kk == 0), stop=(kk == 3))
                    return ins
                pg.add("pe", emit, r=[("wbf", sd[0]), ("wbf", sd[1])] + [("ar", "hid", kk, n) for kk in range(4)],
                       w=[("ps", b)])
                pg.add("dve", (lambda m=m, b=b, n0=n0, nsz=nsz: nc.vector.scalar_tensor_tensor(
                    x[:, m, n0:n0 + nsz], ps[:, b, 0:nsz], 0.5, x[:, m, n0:n0 + nsz], ALU.mult, ALU.add)),
                    r=[("ps", b), ("x", m, n)], w=[("x", m, n)])
                if last and m == KT - 1 and n >= 1:
                    tail(n - 1)
            if last:
                tail(len(NPIECES) - 1)

    def zcols(self, ct, n):
        raise NotImplementedError

    def zview(self, ct, n, sh=2):
        z = self.z
        n0, nsz = NPIECES[n]
        if n < 4:
            return z[:, ct, sh + n0: sh + n0 + nsz]
        v = z[:, ct, 2 + NP: NTZ].rearrange("p (s t) -> p s t", t=LS + 2)
        return v[:, :, sh:sh + LS]

    def pview(self, ap2d, n):
        if n < 4:
            return ap2d
        return ap2d.rearrange("p (s t) -> p s t", t=LS)

    def stage_conv(self):
        nc = self.nc
        pg = self.pg
        x, xn, z, ps = self.x, self.xn, self.z, self.ps
        convw = self.convw
        self.arena_fence(extra=[("dm",)])
        for ct in range(KT):
            pg.add("pool", (lambda ct=ct: nc.gpsimd.memset(z[:, ct, 0:2], 0.0)), w=[("ar", "z", ct, "h")])
        self.dma(self.cachef[:].rearrange("p a b c -> p (a b c)"), self.d_cache, w=[("cachef",)])
        for ct in range(KT):
            def emit(ct=ct):
                v = z[:, ct, 2 + NP: NTZ].rearrange("p (s t) -> p s t", t=LS + 2)
                return nc.vector.tensor_copy(v[:, :, 0:2], self.cachef[:, ct, :, :])
            pg.add("dve", emit, r=[("cachef",)], w=[("ar", "z", ct, "hs")])
        win = self.d_win
        for j in range(KT):
            s = self.load_piece(self.pair_piece_ap(win, D + j * 128, 2 * D + j * 128), self.pair_view)
            for n, (n0, nsz) in enumerate(NPIECES):
                bgc, bv = self.pair_matmuls(
                    s, lambda k, n0, nsz: xn[:, k, n0:n0 + nsz],
                    lambda n: [("xn", k, n) for k in range(KT)], n, n0, nsz)
                t = self.tslot()
                tm = self.tmp[t]
                pg.add("act", (lambda bv=bv, tm=tm, nsz=nsz: nc.scalar.activation(
                    tm[:, 0:nsz], ps[:, bv, 0:nsz], AF.Copy)), r=[("ps", bv)], w=[("tmp", t)])
                pg.add("dve", (lambda bgc=bgc, tm=tm, j=j, n=n, nsz=nsz: nc.vector.tensor_tensor(
                    self.zview(j, n), self.pview(ps[:, bgc, 0:nsz], n), self.pview(tm[:, 0:nsz], n), ALU.mult)),
                    r=[("ps", bgc), ("tmp", t)], w=[("ar", "z", j, n)])
                if n == 3:
                    pg.add("dve", (lambda bgc=bgc, tm=tm, j=j: nc.vector.tensor_tensor(
                        self.zl_p[:, j, :], ps[:, bgc, 510:512], tm[:, 510:512], ALU.mult)),
                        r=[("ps", bgc), ("tmp", t)], w=[("zl_p", j)])
                if n == 4:
                    pg.add("dve", (lambda bgc=bgc, tm=tm, j=j: nc.vector.tensor_tensor(
                        self.zl_s[:, j, :, :], self.pview(ps[:, bgc, 0:128], 4)[:, :, 6:8],
                        self.pview(tm[:, 0:128], 4)[:, :, 6:8], ALU.mult)),
                        r=[("ps", bgc), ("tmp", t)], w=[("zl_s", j)])
        idb = self.identb
        for jp in range(KT // 2):
            s = self.load_piece(self.pair_piece_ap(win, (2 * jp) * 128, (2 * jp + 1) * 128), self.pair_view)
            for h in range(2):
                j = 2 * jp + h
                cd = self.cdiag[j % 2]
                cdk = ("dm", "cd", j % 2)
                for kk in range(3):
                    pg.add("dve", (lambda kk=kk, j=j, cd=cd: nc.vector.tensor_scalar(
                        cd[:, kk, :], idb[:], convw[:, kk, j:j + 1], None, ALU.mult)),
                        r=[("identb",), ("convw",)], w=[cdk + (kk,)])
                for n in (4, 3, 2, 1, 0):
                    n0, nsz = NPIECES[n]
                    b = self.bank()
                    wv = self.wbf[s][:, :].rearrange("p (h k c) -> p h k c", h=2, k=KT)

                    def emit(h=h, b=b, n0=n0, nsz=nsz, wv=wv):
                        ins = None
                        for k in range(KT):
                            ins = nc.tensor.matmul(ps[:, b, 0:nsz], wv[:, h, k, :], xn[:, k, n0:n0 + nsz],
                                                   start=(k == 0), stop=(k == KT - 1))
                        return ins
                    pg.add("pe", emit, r=[("wbf", s)] + [("xn", k, n) for k in range(KT)], w=[("ps", b)])
                    bc = self.bank()
                    zk = [("ar", "z", j, n), ("ar", "z", j, "h"), ("ar", "z", j, "hs")] + ([("ar", "z", j, n - 1)] if 0 < n < 4 else [])

                    def emitc(j=j, n=n, bc=bc, nsz=nsz, cd=cd):
                        ins = None
                        for kk in range(3):
                            ins = nc.tensor.matmul(self.pview(ps[:, bc, 0:nsz], n), cd[:, kk, :], self.zview(j, n, kk),
                                                   start=(kk == 0), stop=(kk == 2))
                        return ins
                    pg.add("pe", emitc, r=zk + [cdk], w=[("ps", bc)])
                    t = self.tslot()
                    tm = self.tmpb[t]
                    pg.add("act", (lambda bc=bc, tm=tm, nsz=nsz: nc.scalar.activation(
                        tm[:, 0:nsz], ps[:, bc, 0:nsz], AF.Copy)), r=[("ps", bc)], w=[("tmp", t)])
                    pg.add("dve", (lambda j=j, n=n, b=b, tm=tm, nsz=nsz: nc.vector.tensor_tensor(
                        self.zview(j, n), self.pview(ps[:, b, 0:nsz], n), self.pview(tm[:, 0:nsz], n), ALU.mult)),
                        r=[("ps", b), ("tmp", t)], w=[("ar", "z", j, n)])
        wout = self.d_wout.rearrange("(k p) n -> p k n", p=128)
        for mp in range(KT // 2):
            s = self.load_piece(wout[:, :, mp * 256:(mp + 1) * 256],
                                lambda a: a.rearrange("p (k c) -> p k c", k=KT))
            wv = self.wbf[s][:, :].rearrange("p (k c) -> p k c", k=KT)
            for h in range(2):
                m = 2 * mp + h
                for n, (n0, nsz) in enumerate(NPIECES):
                    b = self.bank()

                    def emit(h=h, b=b, n=n, nsz=nsz, wv=wv):
                        ins = None
                        for k in range(KT):
                            ins = nc.tensor.matmul(self.pview(ps[:, b, 0:nsz], n), wv[:, k, h * 128:(h + 1) * 128],
                                                   self.zview(k, n), start=(k == 0), stop=(k == KT - 1))
                        return ins
                    pg.add("pe", emit, r=[("wbf", s)] + [("ar", "z", k, n) for k in range(KT)], w=[("ps", b)])
                    pg.add("dve", (lambda m=m, b=b, n0=n0, nsz=nsz: nc.vector.tensor_tensor(
                        x[:, m, n0:n0 + nsz], ps[:, b, 0:nsz], x[:, m, n0:n0 + nsz], ALU.add)),
                        r=[("ps", b), ("x", m, n)], w=[("x", m, n)])
        self.dma(self.o_conv_p, self.zl_p[:].rearrange("p a b -> p (a b)"), r=[("zl_p",)])
        self.dma(self.o_conv_s, self.zl_s[:].rearrange("p a b c -> p (a b c)"), r=[("zl_s",)])
        self.arena_fence()

    def _small(self, eng, fn, r, w):
        self.pg.add(eng, fn, r=[("s5s", k) for k in r], w=[("s5s", k) for k in w])

    def reduce_angle(self, src, dst, eng="dve"):
        nc = self.nc
        T = self.s5s
        E = nc.vector
        self._small(eng, lambda: E.tensor_scalar(T["k"][:], T[src][:], 1.0 / TWO_PI, MAGIC, ALU.mult, ALU.add),
                    [src], ["k"])
        self._small(eng, lambda: E.tensor_scalar(T["k"][:], T["k"][:], MAGIC, -TWO_PI, ALU.subtract, ALU.mult),
                    ["k"], ["k"])
        self._small(eng, lambda: E.tensor_tensor(T[dst][:], T["k"][:], T[src][:], ALU.add), ["k", src], [dst])

    def stage_s5_consts(self):
        nc = self.nc
        pg = self.pg
        T = self.s5s
        V = nc.vector
        self.dma(T["lre"][:], self.d_lam_re, w=[("s5s", "lre")])
        self.dma(T["lim"][:], self.d_lam_im, w=[("s5s", "lim")])
        self.dma(T["dt"][:], self.d_ldt, w=[("s5s", "dt")])
        sm = self._small
        sm("act", lambda: nc.scalar.activation(T["dt"][:], T["dt"][:], AF.Exp), ["dt"], ["dt"])
        sm("dve", lambda: V.tensor_tensor(T["lr"][:], T["lre"][:], T["dt"][:], ALU.mult), ["lre", "dt"], ["lr"])
        sm("act", lambda: nc.scalar.activation(T["mag"][:], T["lr"][:], AF.Exp), ["lr"], ["mag"])
        sm("dve", lambda: V.tensor_tensor(T["th"][:], T["lim"][:], T["dt"][:], ALU.mult), ["lim", "dt"], ["th"])
        self.reduce_angle("th", "u")
        sm("dve", lambda: V.tensor_scalar(T["thn"][:], T["u"][:], 1.0 / TWO_PI, None, ALU.mult), ["u"], ["thn"])
        sm("act", lambda: nc.scalar.activation(T["sn"][:], T["u"][:], AF.Sin), ["u"], ["sn"])
        sm("act", lambda: nc.scalar.activation(T["au"][:], T["u"][:], AF.Abs), ["u"], ["au"])
        hp = self.halfpi
        pg.add("act", lambda: nc.scalar.activation(T["cs"][:], T["au"][:], AF.Sin, bias=hp[:], scale=-1.0),
               r=[("s5s", "au"), ("halfpi",)], w=[("s5s", "cs")])
        sm("dve", lambda: V.tensor_tensor(T["lbre"][:], T["mag"][:], T["cs"][:], ALU.mult), ["mag", "cs"], ["lbre"])
        sm("dve", lambda: V.tensor_tensor(T["lbim"][:], T["mag"][:], T["sn"][:], ALU.mult), ["mag", "sn"], ["lbim"])
        sm("dve", lambda: V.tensor_scalar(T["nr"][:], T["lbre"][:], -1.0, None, ALU.add), ["lbre"], ["nr"])
        sm("dve", lambda: V.tensor_tensor(T["den"][:], T["lre"][:], T["lre"][:], ALU.mult), ["lre"], ["den"])
        sm("dve", lambda: V.tensor_tensor(T["t1"][:], T["lim"][:], T["lim"][:], ALU.mult), ["lim"], ["t1"])
        sm("dve", lambda: V.tensor_tensor(T["den"][:], T["den"][:], T["t1"][:], ALU.add), ["den", "t1"], ["den"])
        sm("dve", lambda: V.reciprocal(T["rden"][:], T["den"][:]), ["den"], ["rden"])
        sm("dve", lambda: V.tensor_tensor(T["t1"][:], T["nr"][:], T["lre"][:], ALU.mult), ["nr", "lre"], ["t1"])
        sm("dve", lambda: V.tensor_tensor(T["t2"][:], T["lbim"][:], T["lim"][:], ALU.mult), ["lbim", "lim"], ["t2"])
        sm("dve", lambda: V.tensor_tensor(T["t1"][:], T["t1"][:], T["t2"][:], ALU.add), ["t1", "t2"], ["t1"])
        sm("dve", lambda: V.tensor_tensor(T["fre"][:], T["t1"][:], T["rden"][:], ALU.mult), ["t1", "rden"], ["fre"])
        sm("dve", lambda: V.tensor_tensor(T["t1"][:], T["lbim"][:], T["lre"][:], ALU.mult), ["lbim", "lre"], ["t1"])
        sm("dve", lambda: V.tensor_tensor(T["t2"][:], T["nr"][:], T["lim"][:], ALU.mult), ["nr", "lim"], ["t2"])
        sm("dve", lambda: V.tensor_tensor(T["t1"][:], T["t1"][:], T["t2"][:], ALU.subtract), ["t1", "t2"], ["t1"])
        sm("dve", lambda: V.tensor_tensor(T["fim"][:], T["t1"][:], T["rden"][:], ALU.mult), ["t1", "rden"], ["fim"])
        sm("dve", lambda: V.tensor_tensor(T["t1"][:], T["fre"][:], T["fre"][:], ALU.mult), ["fre"], ["t1"])
        sm("dve", lambda: V.tensor_tensor(T["t2"][:], T["fim"][:], T["fim"][:], ALU.mult), ["fim"], ["t2"])
        sm("dve", lambda: V.tensor_tensor(T["t1"][:], T["t1"][:], T["t2"][:], ALU.add), ["t1", "t2"], ["t1"])
        sm("dve", lambda: V.reciprocal(T["t2"][:], T["t1"][:]), ["t1"], ["t2"])
        sm("dve", lambda: V.tensor_tensor(T["fire"][:], T["fre"][:], T["t2"][:], ALU.mult), ["fre", "t2"], ["fire"])
        sm("dve", lambda: V.scalar_tensor_tensor(T["fiim"][:], T["fim"][:], -1.0, T["t2"][:], ALU.mult, ALU.mult),
           ["fim", "t2"], ["fiim"])
        si = self.si
        self.dma(si["re"][:].rearrange("p a b -> p (a b)"), self.d_s0_re, w=[("si", "re")])
        self.dma(si["im"][:].rearrange("p a b -> p (a b)"), self.d_s0_im, w=[("si", "im")])

        def bc(name):
            return T[name][:].unsqueeze(2).to_broadcast([128, NPAIR, NSQ])
        self.cmul_inplace(si, "fire", "fiim")
        sm("dve", lambda: V.memset(T["cre"][:], 0.0), [], ["cre"])
        sm("dve", lambda: V.memset(T["cim"][:], 0.0), [], ["cim"])
        for nm, src in (("re", self.d_bt_re), ("im", self.d_bt_im)):
            ss = self.load_raw(src, lambda a: a, 1024)
            stg = self.stg[ss]
            dst = self.btc[nm]
            pg.add("pool", (lambda dst=dst, stg=stg: nc.gpsimd.tensor_copy(
                dst[:].rearrange("p a b -> p (a b)"), stg[:, 0:1024])), r=[("stg", ss)], w=[("btc", nm)])
        sl = [self.load_raw(src, lambda a: a, 1024) for src in (self.d_ct_re, self.d_ct_im)]
        for hf in range(2):
            qa = slice(16 * hf, 16 * hf + 16)
            cre = self.stg[sl[0]][:, 0:1024].rearrange("p (a b) -> p a b", b=32)[:, qa, :]
            cim = self.stg[sl[1]][:, 0:1024].rearrange("p (a b) -> p a b", b=32)[:, qa, :]

            def bc32(name, qa=qa):
                return T[name][:, qa].unsqueeze(2).to_broadcast([128, 16, 32])
            ta, tb = self.tslot(), self.tslot()
            t0 = self.tmp[ta][:, 0:512].rearrange("p (a b) -> p a b", b=32)
            t1 = self.tmp[tb][:, 0:512].rearrange("p (a b) -> p a b", b=32)
            rk = [("stg", sl[0]), ("stg", sl[1]), ("s5s", "fre"), ("s5s", "fim")]
            k0, k1 = ("tmp", ta), ("tmp", tb)
            pg.add("dve", (lambda t0=t0, cre=cre, bc32=bc32: V.tensor_tensor(t0, cre, bc32("fre"), ALU.mult)),
                   r=rk, w=[k0])
            pg.add("dve", (lambda t1=t1, cim=cim, bc32=bc32: V.tensor_tensor(t1, cim, bc32("fim"), ALU.mult)),
                   r=rk, w=[k1])
            pg.add("dve", (lambda t0=t0, t1=t1, qa=qa: V.tensor_tensor(self.cft["re"][:, qa, :], t0, t1, ALU.subtract)),
                   r=[k0, k1], w=[("cft", "re", hf)])
            pg.add("dve", (lambda t0=t0, cre=cre, bc32=bc32: V.tensor_tensor(t0, cre, bc32("fim"), ALU.mult)),
                   r=rk + [k0], w=[k0])
            pg.add("dve", (lambda t1=t1, cim=cim, bc32=bc32: V.tensor_tensor(t1, cim, bc32("fre"), ALU.mult)),
                   r=rk + [k1], w=[k1])
            pg.add("dve", (lambda t0=t0, t1=t1, qa=qa: V.scalar_tensor_tensor(
                self.cft["im"][:, qa, :], t0, -1.0, t1, ALU.mult, ALU.subtract)),
                r=[k0, k1], w=[("cft", "im", hf)])

    def cmul_inplace(self, S, fre, fim):
        nc = self.nc
        pg = self.pg
        V = nc.vector
        T = self.s5s

        def bc(name):
            return T[name][:].unsqueeze(2).to_broadcast([128, NPAIR, NSQ])
        ta, tb = self.tslot(), self.tslot()
        Ta = self.tmp[ta][:, 0:512].rearrange("p (a b) -> p a b", b=NSQ)
        Tb = self.tmp[tb][:, 0:512].rearrange("p (a b) -> p a b", b=NSQ)
        ka, kb = ("tmp", ta), ("tmp", tb)
        kf = [("s5s", fre), ("s5s", fim)]
        pg.add("dve", lambda: V.tensor_tensor(Ta, S["im"][:], bc(fim), ALU.mult), r=[("si", "im")] + kf, w=[ka])
        pg.add("dve", lambda: V.tensor_tensor(Tb, S["re"][:], bc(fim), ALU.mult), r=[("si", "re")] + kf, w=[kb])
        pg.add("dve", lambda: V.tensor_tensor(S["re"][:], S["re"][:], bc(fre), ALU.mult), r=[("si", "re"), kb] + kf,
               w=[("si", "re")])
        pg.add("dve", lambda: V.tensor_tensor(S["re"][:], S["re"][:], Ta, ALU.subtract), r=[("si", "re"), ka],
               w=[("si", "re")])
        pg.add("dve", lambda: V.tensor_tensor(S["im"][:], S["im"][:], bc(fre), ALU.mult), r=[("si", "im"), ka] + kf,
               w=[("si", "im")])
        pg.add("dve", lambda: V.tensor_tensor(S["im"][:], S["im"][:], Tb, ALU.add), r=[("si", "im"), kb],
               w=[("si", "im")])

    def stage_s5(self):
        nc = self.nc
        pg = self.pg
        V, G, A = nc.vector, nc.gpsimd, nc.scalar
        T = self.s5s
        x, xn, ps = self.x, self.xn, self.ps
        yv = self.rstd
        iota1 = self.iota1()
        hp = self.halfpi
        W = 512
        self.arena_fence()
        for ct in range(KT):
            pg.add("pool", (lambda ct=ct: G.tensor_scalar(self.dm[:, ct, :], self.ident(), self.dvec[:, ct:ct + 1], None, ALU.mult)),
                   r=[("cst",), ("dvec",)], w=[("dm", ct)])

        def tabkey(tp, n):
            return ("stg", tp, "tab", n)

        def pk(st, n):
            return ("ar", "s5", st, n)

        def gen_tables(q, tp):
            tb = self.s5tab[tp]
            qs = slice(q, q + 1)
            ang = tb["ang"]
            ka = tabkey(tp, "ang")
            pg.add("dve", (lambda: V.tensor_scalar(ang[:], iota1, T["thn"][:, qs], MAGIC, ALU.mult, ALU.add)),
                   r=[("cst",), ("s5s", "thn")], w=[ka])
            pg.add("dve", (lambda: V.tensor_scalar(ang[:], ang[:], MAGIC, -TWO_PI, ALU.subtract, ALU.mult)),
                   r=[ka], w=[ka])
            pg.add("dve", (lambda: V.scalar_tensor_tensor(ang[:], iota1, T["u"][:, qs], ang[:], ALU.mult, ALU.add)),
                   r=[("cst",), ("s5s", "u"), ka], w=[ka])
            pg.add("dve", (lambda: V.tensor_scalar(ang[:], ang[:], math.pi, -math.pi, ALU.min, ALU.max)),
                   r=[ka], w=[ka])
            pg.add("act", (lambda: A.activation(tb["sin32"][:], ang[:], AF.Sin)), r=[ka], w=[tabkey(tp, "sin32")])
            pg.add("act", (lambda: A.activation(tb["sin16"][:], ang[:], AF.Sin)), r=[ka], w=[tabkey(tp, "sin16")])
            pg.add("act", (lambda: A.activation(tb["nsl"][:, 0:1], ang[:, W - 1:W], AF.Sin, scale=-1.0)), r=[ka],
                   w=[tabkey(tp, "nsl")])
            pg.add("act", (lambda: A.activation(tb["nsin16"][:], ang[:], AF.Sin, scale=-1.0)), r=[ka],
                   w=[("wbf", tp, "nsin")])
            pg.add("act", (lambda: A.activation(ang[:], ang[:], AF.Abs)), r=[ka], w=[ka])
            pg.add("act", (lambda: A.activation(tb["cos32"][:], ang[:], AF.Sin, bias=hp[:], scale=-1.0)),
                   r=[ka, ("halfpi",)], w=[tabkey(tp, "cos32")])
            pg.add("act", (lambda: A.activation(tb["cos16"][:], ang[:], AF.Sin, bias=hp[:], scale=-1.0)),
                   r=[ka, ("halfpi",)], w=[tabkey(tp, "cos16")])

        def gen_pair_consts(q, ct, qi, pb):
            qs = slice(q, q + 1)
            for nm in ("re", "im"):
                bp = self.bpad[nm][pb]
                src = self.btc[nm]
                pg.add("dve", (lambda bp=bp, src=src: V.tensor_scalar(
                    bp[:], src[:, ct, :], self.rowmask(qi), None, ALU.mult)),
                    r=[("btc", nm), ("cst",)], w=[("bpad", nm, pb)])
            mm = self.magmask[pb]
            pg.add("dve", (lambda: V.tensor_scalar(mm[:], self.mask8(), T["mag"][:, qs], None, ALU.mult)),
                   r=[("cst",), ("s5s", "mag")], w=[("magmask", pb)])

        pieces = []
        pairno = 0
        for ct in range(KT):
            for qi in range(4):
                q = 4 * ct + qi
                for pc in range(5):
                    pieces.append(dict(ct=ct, qi=qi, q=q, pc=pc, tp=pairno % 2, first=(pc == 0),
                                       last_of_ct=(qi == 3 and pc == 4)))
                pairno += 1
        for i, pcd in enumerate(pieces):
            pcd["st"] = i % 2
            pcd["smp"] = (pcd["pc"] == 4)
            pcd["c0"] = NP if pcd["smp"] else pcd["pc"] * W
            pcd["csz"] = NS if pcd["smp"] else W
            pcd["n"] = pcd["pc"]

        def views(pcd):
            smp, csz = pcd["smp"], pcd["csz"]
            tb = self.s5tab[pcd["tp"]]
            S = self.s5p[pcd["st"]]
            if smp:
                def v3(a):
                    return a.rearrange("p (s t) -> p s t", t=LS)

                def tabv(t):
                    a = t[:, 0:LS]
                    return bass.AP(a.tensor, a.offset, [list(a.ap[0]), [0, NSQ], [1, LS]])
            else:
                def v3(a):
                    return a

                def tabv(t):
                    return t[:, 0:csz]
            return tb, S, v3, tabv

        def P1a(pcd):
            ct, q, qi, st, tp, smp, c0, csz, n = (pcd[k] for k in ("ct", "q", "qi", "st", "tp", "smp", "c0", "csz", "n"))
            pb = tp
            tb, S, v3, tabv = views(pcd)
            bre, bim = self.bank(), self.bank()
            for nm, b0 in (("re", bre), ("im", bim)):
                bp = self.bpad[nm][pb]
                pg.add("pe", (lambda bp=bp, b0=b0: nc.tensor.matmul(ps[:, b0, 0:csz], bp[:], xn[:, ct, c0:c0 + csz],
                                                                   start=True, stop=True)),
                       r=[("bpad", nm, pb), ("xn", ct, n)], w=[("ps", b0)])
            pg.add("act", (lambda: A.activation(S["pr"][:, 0:csz], ps[:, bre, 0:csz], AF.Copy)), r=[("ps", bre)], w=[pk(st, "pr")])
            pg.add("act", (lambda: A.activation(S["pi"][:, 0:csz], ps[:, bim, 0:csz], AF.Copy)), r=[("ps", bim)], w=[pk(st, "pi")])

        def P1b(pcd):
            ct, q, qi, st, tp, smp, c0, csz, n = (pcd[k] for k in ("ct", "q", "qi", "st", "tp", "smp", "c0", "csz", "n"))
            tb, S, v3, tabv = views(pcd)
            pr, pi, q1, q2, q3, q4 = (v3(S[k][:, 0:csz]) for k in ("pr", "pi", "q1", "q2", "q3", "q4"))
            cw, sw = tabv(tb["cos16"]), tabv(tb["sin16"])
            kc, ks = tabkey(tp, "cos16"), tabkey(tp, "sin16")
            if smp:
                pg.add("dve", (lambda: V.tensor_tensor(q1, pr, cw, ALU.mult)), r=[pk(st, "pr"), kc], w=[pk(st, "q1")])
                pg.add("dve", (lambda: V.tensor_tensor(q2, pi, sw, ALU.mult)), r=[pk(st, "pi"), ks], w=[pk(st, "q2")])
                pg.add("dve", (lambda: V.tensor_tensor(q3, pi, cw, ALU.mult)), r=[pk(st, "pi"), kc], w=[pk(st, "q3")])
                pg.add("dve", (lambda: V.tensor_tensor(q4, pr, sw, ALU.mult)), r=[pk(st, "pr"), ks], w=[pk(st, "q4")])
            if not smp:
                nsw = tabv(tb["nsin16"])
                kn = ("wbf", tp, "nsin")
                pg.add("dve", (lambda: V.tensor_tensor(q1, pr, cw, ALU.mult)), r=[pk(st, "pr"), kc], w=[pk(st, "q1")])
                pg.add("dve", (lambda: V.tensor_tensor(q2, pi, sw, ALU.mult)), r=[pk(st, "pi"), ks], w=[pk(st, "q2")])
                pg.add("dve", (lambda: V.tensor_tensor(q3, pi, cw, ALU.mult)), r=[pk(st, "pi"), kc], w=[pk(st, "q3")])
                pg.add("dve", (lambda: V.tensor_tensor(q4, pr, nsw, ALU.mult)), r=[pk(st, "pr"), kn], w=[pk(st, "q4")])
                sre, sim_ = self.bank(), self.bank()
                pcd["sbanks"] = (sre, sim_)
                idb = self.identb
                for bnk, ka, kb in ((sre, "q1", "q2"), (sim_, "q3", "q4")):
                    def emits(bnk=bnk, ka=ka, kb=kb):
                        nc.tensor.matmul(ps[:, bnk, 0:csz], idb[:], S[ka][:, 0:csz], start=True, stop=False)
                        return nc.tensor.matmul(ps[:, bnk, 0:csz], idb[:], S[kb][:, 0:csz], start=False, stop=True)
                    pg.add("pe", emits, r=[pk(st, ka), pk(st, kb), ("identb",)], w=[("ps", bnk)])
                return
            if smp:
                f1, f3 = v3(S["f1"][:, 0:csz]), v3(S["f3"][:, 0:csz])
                pg.add("dve", (lambda: V.tensor_tensor(f1, q1, q2, ALU.add)), r=[pk(st, "q1"), pk(st, "q2")], w=[pk(st, "A2")])
                pg.add("dve", (lambda: V.tensor_tensor(f3, q3, q4, ALU.subtract)), r=[pk(st, "q3"), pk(st, "q4")], w=[pk(st, "A4")])
            else:
                pg.add("dve", (lambda: V.tensor_tensor(q1, q1, q2, ALU.add)), r=[pk(st, "q1"), pk(st, "q2")], w=[pk(st, "q1")])
                pg.add("dve", (lambda: V.tensor_tensor(q3, q3, q4, ALU.subtract)), r=[pk(st, "q3"), pk(st, "q4")], w=[pk(st, "q3")])

        def P2(pcd):
            ct, q, qi, st, tp, smp, c0, csz, n = (pcd[k] for k in ("ct", "q", "qi", "st", "tp", "smp", "c0", "csz", "n"))
            pb = tp
            qs = slice(q, q + 1)
            tb, S, v3, tabv = views(pcd)
            A1, A2, A3, A4 = S["q1"], S["A2"], S["q3"], S["A4"]
            a1, a2, a3, a4 = (v3(S[k][:, 0:csz]) for k in ("q1", "A2", "q3", "A4"))
            kc, ks = tabkey(tp, "cos32"), tabkey(tp, "sin32")
            if smp:
                si = self.si
                mm = self.magmask[pb]
                F1, F3 = S["f1"], S["f3"]
                f1v, f3v = v3(F1[:, 0:csz]), v3(F3[:, 0:csz])
                for Av, nm, kk in ((f1v, "re", "A2"), (f3v, "im", "A4")):
                    pg.add("dve", (lambda Av=Av, nm=nm: V.scalar_tensor_tensor(
                        Av[:, :, 0], si[nm][:, q, :], T["mag"][:, qs], Av[:, :, 0], ALU.mult, ALU.add)),
                        r=[("si", nm, q), ("s5s", "mag"), pk(st, kk)], w=[pk(st, kk)])
                pg.add("dve", (lambda: V.tensor_tensor_scan(A2[:, 0:NS], mm[:], F1[:, 0:NS], 0.0, ALU.mult, ALU.add)),
                       r=[pk(st, "A2"), ("magmask", pb)], w=[pk(st, "A2")])
                pg.add("dve", (lambda: V.tensor_tensor_scan(A4[:, 0:NS], mm[:], F3[:, 0:NS], 0.0, ALU.mult, ALU.add)),
                       r=[pk(st, "A4"), ("magmask", pb)], w=[pk(st, "A4")])
            else:
                mb = T["mag"][:, qs].to_broadcast([128, csz])
                sre, sim_ = pcd["sbanks"]
                pg.add("dve", (lambda: V.tensor_tensor_scan(A2[:, 0:csz], mb, ps[:, sre, 0:csz], T["cre"][:, qs], ALU.mult, ALU.add)),
                       r=[("ps", sre), ("s5s", "mag"), ("s5s", "cre", q)], w=[pk(st, "A2")])
                pg.add("dve", (lambda: V.tensor_tensor_scan(A4[:, 0:csz], mb, ps[:, sim_, 0:csz], T["cim"][:, qs], ALU.mult, ALU.add)),
                       r=[("ps", sim_), ("s5s", "mag"), ("s5s", "cim", q)], w=[pk(st, "A4")])
            pg.add("act", (lambda: A.activation(S["g1"][:, 0:csz], A2[:, 0:csz], AF.Copy)), r=[pk(st, "A2")], w=[pk(st, "g1")])
            pg.add("act", (lambda: A.activation(S["g2"][:, 0:csz], A4[:, 0:csz], AF.Copy)), r=[pk(st, "A4")], w=[pk(st, "g2")])
            cy = self.cty[st]
            if smp:
                L = LS - 1
                cl, sl_ = tb["cos32"][:, L:L + 1], tb["sin32"][:, L:L + 1]
                t1 = cy[:, 0:1]
                sc = self.scr[st][:, 0:NSQ]
                pg.add("dve", (lambda: V.tensor_scalar(sc, a4[:, :, L], sl_, None, ALU.mult)),
                       r=[pk(st, "A4"), ks], w=[("scr", st, 0)])
                pg.add("dve", (lambda: V.scalar_tensor_tensor(self.si["re"][:, q, :], a2[:, :, L], cl, sc, ALU.mult, ALU.subtract)),
                       r=[pk(st, "A2"), kc, ("scr", st, 0)], w=[("si", "re", q)])
                sc2 = self.scr[st][:, NSQ:2 * NSQ]
                pg.add("dve", (lambda: V.tensor_scalar(sc2, a2[:, :, L], sl_, None, ALU.mult)),
                       r=[pk(st, "A2"), ks], w=[("scr", st, 1)])
                pg.add("dve", (lambda: V.scalar_tensor_tensor(self.si["im"][:, q, :], a4[:, :, L], cl, sc2, ALU.mult, ALU.add)),
                       r=[pk(st, "A4"), kc, ("scr", st, 1)], w=[("si", "im", q)])
            else:
                L = csz - 1
                cl, sl_ = tb["cos32"][:, L:L + 1], tb["sin32"][:, L:L + 1]
                t1, t2 = cy[:, 0:1], cy[:, 1:2]
                nsl = tb["nsl"][:, 0:1]
                pg.add("act", (lambda: A.activation(t1, A4[:, L:L + 1], AF.Copy, scale=nsl)),
                       r=[pk(st, "A4"), tabkey(tp, "nsl")], w=[("cty", st, 0)])
                pg.add("act", (lambda: A.activation(T["cre"][:, qs], A2[:, L:L + 1], AF.Identity, bias=t1, scale=cl)),
                       r=[pk(st, "A2"), kc, ("cty", st, 0)], w=[("s5s", "cre", q)])
                pg.add("act", (lambda: A.activation(t2, A2[:, L:L + 1], AF.Copy, scale=sl_)),
                       r=[pk(st, "A2"), ks], w=[("cty", st, 1)])
                pg.add("act", (lambda: A.activation(T["cim"][:, qs], A4[:, L:L + 1], AF.Identity, bias=t2, scale=cl)),
                       r=[pk(st, "A4"), kc, ("cty", st, 1)], w=[("s5s", "cim", q)])

        def P3(pcd):
            ct, q, qi, st, tp, smp, c0, csz, n = (pcd[k] for k in ("ct", "q", "qi", "st", "tp", "smp", "c0", "csz", "n"))
            tb, S, v3, tabv = views(pcd)
            g1, g2, m1, m2, m3, m4 = (v3(S[k][:, 0:csz]) for k in ("g1", "g2", "m1", "m2", "m3", "m4"))
            cw, sw, nsw = tabv(tb["cos16"]), tabv(tb["sin16"]), tabv(tb["nsin16"])
            kc, ks, kn = tabkey(tp, "cos16"), tabkey(tp, "sin16"), ("wbf", tp, "nsin")
            pg.add("dve", (lambda: V.tensor_tensor(m1, g1, cw, ALU.mult)), r=[pk(st, "g1"), kc], w=[pk(st, "m1")])
            pg.add("dve", (lambda: V.tensor_tensor(m2, g2, nsw, ALU.mult)), r=[pk(st, "g2"), kn], w=[pk(st, "m2")])
            pg.add("dve", (lambda: V.tensor_tensor(m3, g1, sw, ALU.mult)), r=[pk(st, "g1"), ks], w=[pk(st, "m3")])
            pg.add("dve", (lambda: V.tensor_tensor(m4, g2, cw, ALU.mult)), r=[pk(st, "g2"), kc], w=[pk(st, "m4")])
            by = self.bank()
            r0 = 32 * qi

            def emitc():
                o = ps[r0:r0 + 32, by, 0:csz]
                tp_ = (0, r0)
                nc.tensor.matmul(o, self.cft["re"][:, q, :], S["m1"][:, 0:csz], start=True, stop=False, tile_position=tp_)
                nc.tensor.matmul(o, self.cft["re"][:, q, :], S["m2"][:, 0:csz], start=False, stop=False, tile_position=tp_)
                nc.tensor.matmul(o, self.cft["im"][:, q, :], S["m3"][:, 0:csz], start=False, stop=False, tile_position=tp_)
                nc.tensor.matmul(o, self.cft["im"][:, q, :], S["m4"][:, 0:csz], start=False, stop=False, tile_position=tp_)
                return nc.tensor.matmul(o, self.dm[:, ct, r0:r0 + 32], xn[:, ct, c0:c0 + csz],
                                        start=False, stop=True, tile_position=tp_)
            pg.add("pe", emitc, r=[pk(st, "m1"), pk(st, "m2"), pk(st, "m3"), pk(st, "m4"), ("cft", "re"), ("cft", "im"),
                                   ("dm", ct), ("xn", ct, n)], w=[("ps", by)])
            pg.add("act", (lambda: A.activation(yv[r0:r0 + 32, c0:c0 + csz], ps[r0:r0 + 32, by, 0:csz], AF.Copy)),
                   r=[("ps", by)], w=[("rstd", n, qi)])
            if pcd["last_of_ct"]:
                gelu(ct)

        def gelu(ct):
            for n, (n0, nsz) in enumerate(NPIECES):
                yk = [("rstd", n, qi) for qi in range(4)]
                pg.add("act", (lambda ct=ct, n0=n0, nsz=nsz: A.activation(
                    xn[:, ct, n0:n0 + nsz], yv[:, n0:n0 + nsz], AF.Gelu_apprx_tanh)),
                    r=yk, w=[("xn", ct, n)])

        NPC = len(pieces)

        def pre(pcd):
            if pcd["first"]:
                gen_tables(pcd["q"], pcd["tp"])
                gen_pair_consts(pcd["q"], pcd["ct"], pcd["qi"], pcd["tp"])
            P1a(pcd)
        pre(pieces[0])
        for i in range(NPC + 2):
            if i + 1 < NPC:
                pre(pieces[i + 1])
            if i < NPC:
                P1b(pieces[i])
            if i - 2 >= 0:
                P3(pieces[i - 2])
            if 0 <= i - 1 < NPC:
                P2(pieces[i - 1])
        self.arena_fence(extra=[("stg", 0), ("stg", 1), ("wbf", 0), ("wbf", 1)])
        self.cmul_inplace(self.si, "fre", "fim")
        self.dma(self.o_sre_s, self.si["re"][:].rearrange("p a b -> p (a b)"), r=[("si", "re")])
        self.dma(self.o_sim_s, self.si["im"][:].rearrange("p a b -> p (a b)"), r=[("si", "im")])
        sm = self._small
        sm("dve", lambda: V.tensor_tensor(T["ore"][:], T["cre"][:], T["fre"][:], ALU.mult), ["cre", "fre"], ["ore"])
        sm("dve", lambda: V.tensor_tensor(T["t1"][:], T["cim"][:], T["fim"][:], ALU.mult), ["cim", "fim"], ["t1"])
        sm("dve", lambda: V.tensor_tensor(T["ore"][:], T["ore"][:], T["t1"][:], ALU.subtract), ["ore", "t1"], ["ore"])
        sm("dve", lambda: V.tensor_tensor(T["oim"][:], T["cre"][:], T["fim"][:], ALU.mult), ["cre", "fim"], ["oim"])
        sm("dve", lambda: V.tensor_tensor(T["t1"][:], T["cim"][:], T["fre"][:], ALU.mult), ["cim", "fre", "ore"], ["t1"])
        sm("dve", lambda: V.tensor_tensor(T["oim"][:], T["oim"][:], T["t1"][:], ALU.add), ["oim", "t1"], ["oim"])
        self.dma(self.o_sre_p, T["ore"][:], r=[("s5s", "ore")])
        self.dma(self.o_sim_p, T["oim"][:], r=[("s5s", "oim")])
        wglu = self.d_wglu
        for j in range(KT):
            s = self.load_piece(self.pair_piece_ap(wglu, j * 128, D + j * 128), self.pair_view)
            for n, (n0, nsz) in enumerate(NPIECES):
                ba, bb = self.pair_matmuls(
                    s, lambda k, n0, nsz: xn[:, k, n0:n0 + nsz],
                    lambda n: [("xn", k, n) for k in range(KT)], n, n0, nsz)
                t = self.tslot()
                tm = self.tmp[t]
                pg.add("act", (lambda bb=bb, tm=tm, nsz=nsz: nc.scalar.activation(
                    tm[:, 0:nsz], ps[:, bb, 0:nsz], AF.Sigmoid)), r=[("ps", bb)], w=[("tmp", t)])
                pg.add("dve", (lambda ba=ba, tm=tm, nsz=nsz: V.tensor_tensor(
                    tm[:, 0:nsz], ps[:, ba, 0:nsz], tm[:, 0:nsz], ALU.mult)),
                    r=[("ps", ba), ("tmp", t)], w=[("tmp", t)])
                pg.add("dve", (lambda j=j, tm=tm, n0=n0, nsz=nsz: V.tensor_tensor(
                    x[:, j, n0:n0 + nsz], x[:, j, n0:n0 + nsz], tm[:, 0:nsz], ALU.add)),
                    r=[("tmp", t), ("x", j, n)], w=[("x", j, n)])

    def stage_store(self):
        yT = self.o_yT.rearrange("(k p) n -> p k n", p=128)
        for n, (n0, nsz) in enumerate(NPIECES):
            self.dma(yT[:, :, n0:n0 + nsz], self.x[:, :, n0:n0 + nsz], r=[("x", ct, n) for ct in range(KT)])

    def build(self):
        st = self.stages
        full = "all" in st
        self.stage_load()
        nrm = lambda gi, final=False: (lambda n: self.norm_piece(gi, n, final))
        self.stage_norm(0)
        if full or "s5" in st:
            self.stage_s5_consts()
        if full:
            self.stage_ffn(0, tail=nrm(1))
            self.stage_s5()
            self.stage_norm(2)
            self.stage_ffn(1, tail=nrm(3))
            self.stage_ffn(2, tail=nrm(4))
            self.stage_conv()
            self.stage_norm(5)
            self.stage_ffn(3, tail=nrm(6, True))
        else:
            if "ffn" in st:
                self.stage_ffn(0, tail=nrm(6, True))
            if "s5" in st:
                if "ffn" not in st:
                    pass
                self.stage_norm(1)
                self.stage_s5()
                self.stage_norm(6, final=True)
            if "conv" in st:
                self.stage_norm(4)
                self.stage_conv()
                self.stage_norm(6, final=True)
        self.stage_store()
        self.pg.finalize()
        return self.nc


def _consts():
    c = np.zeros((128, 512 + 128 + 4 + 128), np.float32)
    c[:, 0:512] = np.arange(1, 513, dtype=np.float32)[None, :]
    m8 = np.ones((NSQ, LS), np.float32)
    m8[:, 0] = 0.0
    c[:, 512:640] = m8.reshape(1, 128)
    for i in range(4):
        c[32 * i:32 * i + 32, 640 + i] = 1.0
    c[:, 644:772] = np.eye(128, dtype=np.float32)
    return c


def _chan_layout(v):
    v = np.asarray(v, np.float32)
    lead = v.shape[:-1]
    return np.ascontiguousarray(np.moveaxis(v.reshape(lead + (KT, 128)), -1, 0))


def _pair_layout(a):
    a = np.asarray(a, np.float32).reshape(NPAIR, 2, 64)
    return np.ascontiguousarray(a.transpose(1, 2, 0).reshape(128, NPAIR))


def _shared_inputs(inp):
    sh = {}
    sh["w_gu"] = np.ascontiguousarray(np.asarray(inp["ffn_w_gate_up"], np.float32).reshape(4, D, 2 * DFF))
    sh["w_dn"] = np.ascontiguousarray(np.asarray(inp["ffn_w_down"], np.float32).reshape(4, DFF, D))
    sh["w_glu"] = np.ascontiguousarray(np.asarray(inp["ssm_w_glu"], np.float32)[0])
    sh["w_in"] = np.ascontiguousarray(np.asarray(inp["conv_w_in"], np.float32)[0])
    sh["w_out"] = np.ascontiguousarray(np.asarray(inp["conv_w_out"], np.float32)[0])
    g = np.concatenate([np.asarray(inp["norm_g"], np.float32).reshape(6, D),
                        np.asarray(inp["final_norm_g"], np.float32).reshape(1, D)], axis=0)
    sh["gvec"] = _chan_layout(g).reshape(128, 7 * KT)
    sh["dvec"] = _chan_layout(np.asarray(inp["ssm_d"], np.float32)[0]).reshape(128, KT)
    sh["convw"] = _chan_layout(np.asarray(inp["conv_w"], np.float32)[0]).reshape(128, 3 * KT)
    sh["lamT_re"] = _pair_layout(inp["ssm_lam_re"][0])
    sh["lamT_im"] = _pair_layout(inp["ssm_lam_im"][0])
    ldt = np.repeat(np.asarray(inp["ssm_log_dt"], np.float32)[0][:, None], 64, axis=1)
    sh["ldtT"] = _pair_layout(ldt)
    for nm, key in (("btc_re", "ssm_b_re"), ("btc_im", "ssm_b_im")):
        B = np.asarray(inp[key], np.float32)[0]
        out = np.zeros((128, KT, 2, 64), np.float32)
        Bg = B.reshape(KT, 8, 64, 16)
        for gl in range(8):
            m = gl % 2
            out[16 * gl:16 * gl + 16, :, m, :] = Bg[:, gl, :, :].transpose(2, 0, 1)
        sh[nm] = out.reshape(128, KT * 128)
    for nm, key in (("ctc_re", "ssm_c_re"), ("ctc_im", "ssm_c_im")):
        C = np.asarray(inp[key], np.float32)[0].reshape(NPAIR, 2, 16, 64)
        out = np.zeros((2, 64, NPAIR, 2, 16), np.float32)
        for m in range(2):
            out[m, :, :, m, :] = C[:, m, :, :].transpose(2, 0, 1)
        sh[nm] = out.reshape(128, NPAIR * 32)
    sh["consts"] = _consts()
    return sh


def _core_inputs(inp, c):
    d = {}
    xp = np.asarray(inp["x_prompt"], np.float32)[c]
    xs = np.asarray(inp["x_sample"], np.float32)[NSQ * c:NSQ * (c + 1)].reshape(NS, D)
    d["xT"] = np.ascontiguousarray(np.concatenate([xp, xs], axis=0).T)
    cc = np.asarray(inp["cache_conv"], np.float32)[0, NSQ * c:NSQ * (c + 1)]
    d["cacheT"] = _chan_layout(cc).reshape(128, KT * NSQ * 2) if False else np.ascontiguousarray(
        np.moveaxis(cc.reshape(NSQ, 2, KT, 128), 3, 0).transpose(0, 3, 1, 2)).reshape(128, KT * NSQ * 2)
    for nm, key in (("s0_re", "state_ssm_re"), ("s0_im", "state_ssm_im")):
        s = np.asarray(inp[key], np.float32)[0, NSQ * c:NSQ * (c + 1)]
        s = s.reshape(NSQ, NPAIR, 2, 64).transpose(2, 3, 1, 0)
        d[nm] = np.ascontiguousarray(s).reshape(128, NPAIR * NSQ)
    return d


def _unpair(a):
    tail = a.shape[2:]
    a = a.reshape((2, 64, NPAIR) + tail)
    a = np.moveaxis(a, 2, 0)
    return a.reshape((64, 64) + tail)


_NC_CACHE = {}


def kernel(**inputs):
    if "full" not in _NC_CACHE:
        _NC_CACHE["full"] = Builder().build()
    nc = _NC_CACHE["full"]
    sh = _shared_inputs(inputs)
    in_maps = []
    for c in range(NCORES):
        m = dict(sh)
        m.update(_core_inputs(inputs, c))
        in_maps.append(m)
    res = run_bass_kernel_spmd(nc, in_maps, core_ids=list(range(NCORES)))
    R = res.results
    y_p = np.zeros((NCORES, NP, D), np.float32)
    y_s = np.zeros((NCORES * NSQ, LS, D), np.float32)
    sre_p = np.zeros((1, NCORES, 64, 64), np.float32)
    sim_p = np.zeros((1, NCORES, 64, 64), np.float32)
    conv_p = np.zeros((1, NCORES, 2, D), np.float32)
    sre_s = np.zeros((1, NCORES * NSQ, 64, 64), np.float32)
    sim_s = np.zeros((1, NCORES * NSQ, 64, 64), np.float32)
    conv_s = np.zeros((1, NCORES * NSQ, 2, D), np.float32)
    for c in range(NCORES):
        r = R[c]
        yT = np.asarray(r["yT"])
        y_p[c] = yT[:, :NP].T
        y_s[NSQ * c:NSQ * (c + 1)] = yT[:, NP:].T.reshape(NSQ, LS, D)
        sre_p[0, c] = _unpair(np.asarray(r["sre_p"]).reshape(128, NPAIR))
        sim_p[0, c] = _unpair(np.asarray(r["sim_p"]).reshape(128, NPAIR))
        sre_s[0, NSQ * c:NSQ * (c + 1)] = np.moveaxis(_unpair(np.asarray(r["sre_s"]).reshape(128, NPAIR, NSQ)), 2, 0)
        sim_s[0, NSQ * c:NSQ * (c + 1)] = np.moveaxis(_unpair(np.asarray(r["sim_s"]).reshape(128, NPAIR, NSQ)), 2, 0)
        cp = np.asarray(r["conv_p"]).reshape(128, KT, 2)
        conv_p[0, c] = cp.transpose(2, 1, 0).reshape(2, D)
        cs = np.asarray(r["conv_s"]).reshape(128, KT, NSQ, 2)
        conv_s[0, NSQ * c:NSQ * (c + 1)] = cs.transpose(2, 3, 1, 0).reshape(NSQ, 2, D)
    return (y_p, y_s, sre_p, sim_p, conv_p, sre_s, sim_s, conv_s)
```
